# Optimizing a Trainium2 kernel written in Bass

```python
import math
import jax, jax.numpy as jnp
from jax import lax
import numpy as np

D_MODEL = 1024
BATCH = 16
SEQ = 4096
DEPTH = 2

GRID_W = 64
ATTN_WIDTH = D_MODEL // 2
POOL_WIDTH = D_MODEL - ATTN_WIDTH
HEAD_DIM = 64
N_HEADS = ATTN_WIDTH // HEAD_DIM
WIN_H_MAX = 8
WIN_W = 16
POOL_WINDOWS = (2, 4, 8, 16)
N_POOL_GROUPS = len(POOL_WINDOWS)
POOL_GROUP = POOL_WIDTH // N_POOL_GROUPS
PROJ_WIDTH = 3 * ATTN_WIDTH + POOL_WIDTH
D_FF = 4 * D_MODEL
N_MOD = 6
DN_ALPHA = (2.0 * DEPTH) ** 0.25
DN_BETA = (8.0 * DEPTH) ** -0.25
LN_EPS = 1e-5

kernel_name = "hybrid_natten_pool_deepnorm_encoder"


def layer_norm(x, g, b):
    xf = x.astype(jnp.float32)
    mu = jnp.mean(xf, axis=-1, keepdims=True)
    var = jnp.mean(jnp.square(xf - mu), axis=-1, keepdims=True)
    y = (xf - mu) * lax.rsqrt(var + LN_EPS)
    return (y * g.astype(jnp.float32) + b.astype(jnp.float32)).astype(x.dtype)


def neighborhood_attention(q, k, v, rpb):
    b, s, h, dh = q.shape
    rows = s // GRID_W
    kh = min(WIN_H_MAX, rows)
    qg = q.reshape(b, rows, GRID_W, h, dh).transpose(1, 0, 3, 2, 4)
    kg = k.reshape(b, rows, GRID_W, h, dh).transpose(0, 3, 1, 2, 4)
    vg = v.reshape(b, rows, GRID_W, h, dh).transpose(0, 3, 1, 2, 4)
    col = np.arange(GRID_W)
    col_start = np.clip(col - WIN_W // 2, 0, GRID_W - WIN_W)
    col_idx = col_start[:, None] + np.arange(WIN_W)[None, :]
    col_rel = col_idx - col[:, None] + (WIN_W - 1)
    rpb_c = rpb[:, :, col_rel]
    scale = HEAD_DIM ** -0.5

    def row_block(args):
        r, q_r = args
        start = jnp.clip(r - kh // 2, 0, rows - kh)
        key_rows = start + jnp.arange(kh)
        k_win = jnp.take(jnp.take(kg, key_rows, axis=2), col_idx, axis=3)
        v_win = jnp.take(jnp.take(vg, key_rows, axis=2), col_idx, axis=3)
        bias = jnp.take(rpb_c, key_rows - r + (WIN_H_MAX - 1), axis=1)
        bias = bias.transpose(0, 2, 1, 3).astype(jnp.float32)
        sc = jnp.einsum('bhqd,bhrqcd->bhqrc', q_r * scale, k_win).astype(jnp.float32) + bias[None]
        p = jax.nn.softmax(sc.reshape(b, h, GRID_W, kh * WIN_W), axis=-1)
        p = p.reshape(b, h, GRID_W, kh, WIN_W).astype(v_win.dtype)
        return jnp.einsum('bhqrc,bhrqcd->bhqd', p, v_win)

    out = lax.map(row_block, (jnp.arange(rows), qg))
    return out.transpose(1, 0, 3, 2, 4).reshape(b, s, h * dh)


def multiscale_pool(u, w_pool, pool_scale):
    b, s, _ = u.shape
    uf = u.reshape(b, s, N_POOL_GROUPS, POOL_GROUP).astype(jnp.float32)
    csum = jnp.concatenate([jnp.zeros((b, 1, N_POOL_GROUPS, POOL_GROUP), jnp.float32),
                            jnp.cumsum(uf, axis=1)], axis=1)
    t = np.arange(s)[:, None]
    w = np.array(POOL_WINDOWS)[None, :]
    lo = np.clip(t - w // 2, 0, s)
    hi = np.clip(t - w // 2 + w, 0, s)
    g = np.arange(N_POOL_GROUPS)[None, :]
    window_sum = csum[:, hi, g] - csum[:, lo, g]
    count = (hi - lo).astype(np.float32)[None, :, :, None]
    mixed = (window_sum / count - uf).astype(u.dtype)
    y = jnp.einsum('bsgc,gcd->bsgd', mixed, w_pool).reshape(b, s, POOL_WIDTH)
    return y * pool_scale


def setup_inputs(seed: int = 0) -> dict:
    key = jax.random.key(seed)
    ks = jax.random.split(key, 20)
    f32 = jnp.float32
    x = jax.random.normal(ks[0], (BATCH, SEQ, D_MODEL), f32)
    c = jax.random.normal(ks[1], (BATCH, D_MODEL), f32)
    ln_in_g = 1.0 + 0.02 * jax.random.normal(ks[2], (D_MODEL,), f32)
    ln_in_b = 0.02 * jax.random.normal(ks[3], (D_MODEL,), f32)
    w_ada = 0.1 * D_MODEL ** -0.5 * jax.random.normal(ks[4], (DEPTH, D_MODEL, N_MOD * D_MODEL), f32)
    b_ada = 0.01 * jax.random.normal(ks[5], (DEPTH, N_MOD * D_MODEL), f32)
    col_scale = np.concatenate([np.ones(2 * ATTN_WIDTH, np.float32),
                                np.full(ATTN_WIDTH, DN_BETA, np.float32),
                                np.ones(POOL_WIDTH, np.float32)])
    w_in = D_MODEL ** -0.5 * jax.random.normal(ks[6], (DEPTH, D_MODEL, PROJ_WIDTH), f32) * col_scale
    rpb = 0.1 * jax.random.normal(ks[7], (DEPTH, N_HEADS, 2 * WIN_H_MAX - 1, 2 * WIN_W - 1), f32)
    w_pool = POOL_GROUP ** -0.5 * jax.random.normal(ks[8], (DEPTH, N_POOL_GROUPS, POOL_GROUP, POOL_GROUP), f32)
    pool_scale = 1.0 + 0.1 * jax.random.normal(ks[9], (DEPTH, POOL_WIDTH), f32)
    w_out = DN_BETA * D_MODEL ** -0.5 * jax.random.normal(ks[10], (DEPTH, ATTN_WIDTH + POOL_WIDTH, D_MODEL), f32)
    ln1_g = 1.0 + 0.02 * jax.random.normal(ks[11], (DEPTH, D_MODEL), f32)
    ln1_b = 0.02 * jax.random.normal(ks[12], (DEPTH, D_MODEL), f32)
    w_mlp1 = D_MODEL ** -0.5 * jax.random.normal(ks[13], (DEPTH, D_MODEL, D_FF), f32)
    w_mlp2 = DN_BETA * D_FF ** -0.5 * jax.random.normal(ks[14], (DEPTH, D_FF, D_MODEL), f32)
    ln2_g = 1.0 + 0.02 * jax.random.normal(ks[15], (DEPTH, D_MODEL), f32)
    ln2_b = 0.02 * jax.random.normal(ks[16], (DEPTH, D_MODEL), f32)
    return {"x": x, "c": c, "ln_in_g": ln_in_g, "ln_in_b": ln_in_b, "w_ada": w_ada, "b_ada": b_ada,
            "w_in": w_in, "rpb": rpb, "w_pool": w_pool, "pool_scale": pool_scale, "w_out": w_out,
            "ln1_g": ln1_g, "ln1_b": ln1_b, "w_mlp1": w_mlp1, "w_mlp2": w_mlp2,
            "ln2_g": ln2_g, "ln2_b": ln2_b}


def reference(x, c, ln_in_g, ln_in_b, w_ada, b_ada, w_in, rpb, w_pool, pool_scale, w_out,
              ln1_g, ln1_b, w_mlp1, w_mlp2, ln2_g, ln2_b):
    b, s, _ = x.shape
    x = layer_norm(x, ln_in_g, ln_in_b)
    c_act = jax.nn.silu(c)
    for l in range(DEPTH):
        mod = (c_act @ w_ada[l] + b_ada[l])[:, None, :]
        sh_a, sc_a, g_a, sh_m, sc_m, g_m = jnp.split(mod, N_MOD, axis=-1)

        h = x * (1.0 + sc_a) + sh_a
        proj = h @ w_in[l]
        q, k, v, u = jnp.split(proj, [ATTN_WIDTH, 2 * ATTN_WIDTH, 3 * ATTN_WIDTH], axis=-1)
        q = q.reshape(b, s, N_HEADS, HEAD_DIM)
        k = k.reshape(b, s, N_HEADS, HEAD_DIM)
        v = v.reshape(b, s, N_HEADS, HEAD_DIM)
        y_attn = neighborhood_attention(q, k, v, rpb[l])
        y_pool = multiscale_pool(u, w_pool[l], pool_scale[l])
        y = jnp.concatenate([y_attn, y_pool], axis=-1) @ w_out[l]
        x = layer_norm(DN_ALPHA * x + (1.0 + g_a) * y, ln1_g[l], ln1_b[l])

        h = x * (1.0 + sc_m) + sh_m
        f = jnp.square(jax.nn.relu(h @ w_mlp1[l])) @ w_mlp2[l]
        x = layer_norm(DN_ALPHA * x + (1.0 + g_m) * f, ln2_g[l], ln2_b[l])
    return x
```

```python
import numpy as np
import concourse.bass as bass
import concourse.mybir as mybir
from concourse.bass_utils import run_bass_kernel_spmd

F32 = mybir.dt.float32
BF16 = mybir.dt.bfloat16
AF = mybir.ActivationFunctionType
ALU = mybir.AluOpType

NCORES = 8
D = 1024
SEQ = 4096
NSEQ = 2
T = NSEQ * SEQ
L = 2
ALPHA = (2.0 * L) ** 0.25
EPS_P = 1e-5 / (ALPHA * ALPHA)
NEG = -80.0
SBUF_WORDS = 52500


class Buf:
    __slots__ = ("name", "w", "r", "dsem", "dcnt")

    def __init__(self, name):
        self.name = name
        self.w = None
        self.r = {}
        self.dsem = None
        self.dcnt = 0


class Prog:
    def __init__(self, nc):
        self.nc = nc
        self.eng = {"pe": nc.tensor, "act": nc.scalar, "dve": nc.vector, "pool": nc.gpsimd, "sp": nc.sync}
        self.semobj = {}
        self.cnt = {}
        for k in ("pe", "act", "dve", "pool"):
            self.semobj[k] = nc.alloc_semaphore("s_" + k)
            self.cnt[k] = 0
        self.seen = {k: {} for k in self.eng}
        self.dbufs = []
        self.nbuf = 0

    def buf(self, name):
        self.nbuf += 1
        return Buf("%s_%d" % (name, self.nbuf))

    def _deps(self, e, reads, writes):
        need = {}

        def add(k, v):
            if e == "pe" and k == "pe":
                return
            if need.get(k, 0) < v:
                need[k] = v

        for b in reads:
            if b.w is not None:
                add(*b.w)
        for b in writes:
            if b.w is not None:
                add(*b.w)
            for k, v in b.r.items():
                add(k, v)
        seen = self.seen[e]
        for k, v in need.items():
            if seen.get(k, 0) >= v:
                continue
            self.eng[e].wait_ge(self.semobj[k], v)
            seen[k] = v

    def _mark(self, ev, reads, writes):
        k, v = ev
        for b in reads:
            if b.r.get(k, 0) < v:
                b.r[k] = v
        for b in writes:
            b.w = ev
            b.r = {}

    def op(self, e, fn, reads=(), writes=()):
        self._deps(e, reads, writes)
        ins = fn(self.eng[e])
        self.cnt[e] += 1
        ins.then_inc(self.semobj[e], 1)
        self._mark((e, self.cnt[e]), reads, writes)

    def dma(self, q, pairs, reads, writes, sembuf):
        self._deps(q, reads, writes)
        if sembuf.dsem is None:
            sembuf.dsem = self.nc.alloc_semaphore("d_" + sembuf.name)
            self.semobj["d:" + sembuf.name] = sembuf.dsem
            self.dbufs.append(sembuf)
        for o, i in pairs:
            ins = self.eng[q].dma_start(out=o, in_=i)
            sembuf.dcnt += 16
            ins.then_inc(sembuf.dsem, 16)
        self._mark(("d:" + sembuf.name, sembuf.dcnt), reads, writes)

    def barrier(self, engines=("pe", "act", "dve", "pool", "sp")):
        evs = [(k, self.cnt[k]) for k in ("pe", "act", "dve", "pool") if self.cnt[k] > 0]
        evs += [("d:" + b.name, b.dcnt) for b in self.dbufs if b.dcnt > 0]
        for e in engines:
            seen = self.seen[e]
            for k, v in evs:
                if seen.get(k, 0) >= v:
                    continue
                self.eng[e].wait_ge(self.semobj[k], v)
                seen[k] = v


class Arena:
    def __init__(self, big):
        self.big = big
        self.off = 0

    def f32(self, n):
        a = self.big[:, self.off:self.off + n]
        self.off += n
        assert self.off <= SBUF_WORDS, self.off
        return a

    def bf16(self, n):
        w = (n + 1) // 2
        a = self.big[:, self.off:self.off + w].bitcast(BF16)
        self.off += w
        assert self.off <= SBUF_WORDS, self.off
        return a


def build_nc(debug=False, stop_after=None, lim=None):
    nc = bass.Bass("TRN2", target_bir_lowering=False)
    P = Prog(nc)

    def din(name, shape, dt=F32):
        return nc.dram_tensor(name, list(shape), dt, kind="ExternalInput").ap()

    xT_d = din("xT", [D, T])
    cT_d = din("cT", [128, 8, NSEQ])
    wada_d = din("w_ada", [L, 128, 8, 6 * D])
    bada_d = din("b_ada", [128, L, 48])
    win_d = din("w_in", [L, 128, 8, 2048])
    wout_d = din("w_out", [L, 128, 8, 1024])
    w1_d = din("w_mlp1", [L, 128, 8, 4096])
    w2_d = din("w_mlp2", [L, 128, 32, 1024])
    wpool_d = din("w_pool", [L, 128, 4, 128])
    vecs_d = din("vecs", [128, 88])
    biasT_d = din("biasT", [L, 2, 128, 8192])
    ident_d = din("ident", [128, 128])
    invc_d = din("invc", [128, 64])
    outT_d = nc.dram_tensor("outT", [D, T], F32, kind="ExternalOutput").ap()
    skind = "ExternalOutput" if debug else "Internal"
    X_d = nc.dram_tensor("Xs", [D, T], F32, kind=skind).ap()
    Q_d = nc.dram_tensor("Qs", [512, T], BF16, kind=skind).ap()
    K_d = nc.dram_tensor("Ks", [512, T], BF16, kind=skind).ap()
    U_d = nc.dram_tensor("Us", [512, T], F32, kind=skind).ap()
    V_d = nc.dram_tensor("Vs", [T, 520], BF16, kind=skind).ap()

    xT_v = xT_d.rearrange("(m p) t -> p m t", p=128)
    outT_v = outT_d.rearrange("(m p) t -> p m t", p=128)
    X_v = X_d.rearrange("(m p) t -> p m t", p=128)
    Q_v = Q_d.rearrange("(m p) t -> p m t", p=128)
    K_v = K_d.rearrange("(m p) t -> p m t", p=128)
    U_v = U_d.rearrange("(m p) t -> p m t", p=128)
    V_v = V_d.rearrange("(n p) f -> p n f", p=128)

    big = nc.alloc_sbuf_tensor("big", [128, SBUF_WORDS], F32)
    psA = nc.alloc_psum_tensor("psA", [128, 3584], F32)
    psT = nc.alloc_psum_tensor("psT", [128, 1024], BF16)
    A = Arena(big)

    NB = T // 512
    Xb = [P.buf("Xd") for _ in range(NB)]
    Qb = [P.buf("Qd") for _ in range(NB)]
    Kb = [P.buf("Kd") for _ in range(NB)]
    Ub = [P.buf("Ud") for _ in range(NB)]
    Vb = [P.buf("Vd") for _ in range(NB)]

    psb = [P.buf("ps") for _ in range(7)]
    pstb = P.buf("pst")

    def bank(i, n=512):
        return psA[:, i * 512:i * 512 + n]

    ones_bf = A.bf16(128)
    ident_bf = A.bf16(128)
    vecs = A.f32(88)
    invc = A.f32(64)
    modv = A.f32(L * 48 * NSEQ)
    cact = A.f32(8 * NSEQ)
    bada = A.f32(L * 48)
    const_b = P.buf("const")
    mod_b = P.buf("mod")
    epsv = A.f32(2)
    ident_st = A.f32(128)
    PERSIST = A.off

    modv4 = modv.rearrange("p (l j s) -> p l j s", l=L, j=48)

    def mv(l, j, s):
        return modv4[:, l, j, s:s + 1]

    def vec(i):
        return vecs[:, i:i + 1]
    P.dma("sp", [(vecs, vecs_d[:, :]), (invc, invc_d[:, :]), (cact, cT_d.rearrange("p k s -> p (k s)")),
                 (bada, bada_d.rearrange("p l j -> p (l j)"))], [], [const_b], const_b)
    P.dma("sp", [(ident_st, ident_d[:, :])], [], [const_b], const_b)
    P.op("dve", lambda e: e.tensor_copy(out=ident_bf, in_=ident_st), [const_b], [const_b])
    P.op("dve", lambda e: e.memset(ones_bf, 1.0 / D), [], [const_b])
    P.op("act", lambda e: e.activation(out=cact, in_=cact, func=AF.Silu), [const_b], [const_b])

    class Deferred:
        def __init__(self):
            self.q = []

        def add(self, fn):
            self.q.append(fn)

        def pop(self, n=1):
            for _ in range(n):
                if self.q:
                    self.q.pop(0)()

        def flush(self):
            while self.q:
                self.q.pop(0)()

    def bufs(name, n):
        return [P.buf(name) for _ in range(n)]

    P.op("dve", lambda e: e.memset(epsv[:, 0:1], 1e-5), [], [const_b])
    P.op("dve", lambda e: e.memset(epsv[:, 1:2], EPS_P), [], [const_b])
    for l in range(L):
        for g, w in enumerate((2, 4, 8, 16)):
            c = 48 + l * 4 + g
            P.op("dve", lambda e, c=c, w=w: e.tensor_scalar(out=vecs[:, c:c + 1], in0=vecs[:, c:c + 1],
                                                            scalar1=1.0 / w, scalar2=None, op0=ALU.mult),
                 [const_b], [const_b])

    def eps_ap(eps):
        return epsv[:, 0:1] if eps == 0 else epsv[:, 1:2]

    st_b = [P.buf("st") for _ in range(4)]

    def ln_steps(z, zb_, N, zbf, zsq, st, gam, bet, eps, out, outb, ps1, ps2, final_steps=()):
        msq, sd, Aa, Bb = st
        stb = st_b
        pre = []
        for m in range(8):
            pre.append(lambda m=m: P.op("act", lambda e: e.activation(
                out=zsq[0][:, m, :], in_=z[:, m, :], func=AF.Square), [zb_[m]], [zsq[1][m]]))
            pre.append(lambda m=m: P.op("dve", lambda e: e.tensor_copy(out=zbf[0][:, m, :], in_=z[:, m, :]),
                                        [zb_[m]], [zbf[1][m]]))

        def l2a():
            def s1(e):
                ins = None
                for m in range(8):
                    ins = e.matmul(bank(ps1)[:, 0:N], ones_bf, zbf[0][:, m, :], start=(m == 0), stop=(m == 7))
                return ins

            def s2(e):
                ins = None
                for m in range(8):
                    ins = e.matmul(bank(ps2)[:, 0:N], ones_bf, zsq[0][:, m, :], start=(m == 0), stop=(m == 7))
                return ins
            P.op("pe", s1, zbf[1] + [const_b], [psb[ps1]])
            P.op("pe", s2, zsq[1] + [const_b], [psb[ps2]])
        pre.append(l2a)
        pre.append(lambda: P.op("act", lambda e: e.activation(out=msq, in_=bank(ps1)[:, 0:N], func=AF.Square),
                                [psb[ps1]], [stb[0]]))
        pre.append(lambda: P.op("dve", lambda e: e.tensor_tensor(out=sd, in0=bank(ps2)[:, 0:N], in1=msq,
                                                                 op=ALU.subtract), [psb[ps2], stb[0]], [stb[1]]))
        pre.append(lambda: P.op("act", lambda e: e.activation(out=sd, in_=sd, func=AF.Ln, bias=eps_ap(eps), scale=1.0),
                                [stb[1], const_b], [stb[1]]))
        pre.append(lambda: P.op("act", lambda e: e.activation(out=Aa, in_=sd, func=AF.Exp, scale=-0.5),
                                [stb[1]], [stb[2]]))
        pre.append(lambda: P.op("dve", lambda e: e.scalar_tensor_tensor(
            out=Bb, in0=bank(ps1)[:, 0:N], scalar=-1.0, in1=Aa, op0=ALU.mult, op1=ALU.mult),
            [psb[ps1], stb[2]], [stb[3]]))
        post = []

        def zA(m):
            P.op("dve", lambda e: e.tensor_tensor(out=z[:, m, :], in0=z[:, m, :], in1=Aa, op=ALU.mult),
                 [zb_[m], stb[2]], [zb_[m]])

        def zB(m):
            P.op("pool", lambda e: e.tensor_tensor(out=z[:, m, :], in0=z[:, m, :], in1=Bb, op=ALU.add),
                 [zb_[m], stb[3]], [zb_[m]])

        def aff(m):
            P.op("act", lambda e: e.activation(out=out[:, m, :], in_=z[:, m, :], func=AF.Identity,
                                               bias=bet(m), scale=gam(m)), [zb_[m], const_b], [outb[m]])
        for t in range(12):
            if t < 8:
                post.append(lambda m=t: zA(m))
            if 2 <= t < 10:
                post.append(lambda m=t - 2: zB(m))
            if 4 <= t < 12:
                post.append(lambda m=t - 4: aff(m))
        post.extend(final_steps)
        return pre, post

    def phase0():
        A.off = PERSIST
        wsl = [A.f32(8 * 512) for _ in range(3)]
        wslb = bufs("wsl", 3)
        cact3 = cact.rearrange("p (k s) -> p k s", k=8)
        modrow = A.f32(L * 6 * D)
        mrb = bufs("modrow", 24)
        MD = Deferred()

        def mstep(l, sl, it):
            wb3 = wsl[it % 3].rearrange("p (k n) -> p k n", k=8)
            wbb = wslb[it % 3]
            P.dma("sp", [(wb3, wada_d[l, :, :, sl * 512:(sl + 1) * 512])], [], [wbb], wbb)
            pbk = 1 + it % 4

            def mm(e):
                ins = None
                for k in range(8):
                    ins = e.matmul(bank(pbk)[0:NSEQ, :], cact3[:, k, :], wb3[:, k, :], start=(k == 0), stop=(k == 7))
                return ins
            P.op("pe", mm, [wbb, const_b], [psb[pbk]])
            o = l * 6 * D + sl * 512
            P.op("dve", lambda e: e.tensor_copy(out=modrow[0:NSEQ, o:o + 512], in_=bank(pbk)[0:NSEQ, :]),
                 [psb[pbk]], [mrb[it]])
        it = 0
        for l in range(L):
            for sl in range(12):
                MD.add(lambda l=l, sl=sl, it=it: mstep(l, sl, it))
                it += 1

        xs = [A.f32(8 * 512).rearrange("p (m t) -> p m t", m=8) for _ in range(2)]
        xsb = [bufs("x0", 8), bufs("x0", 8)]
        os_ = [A.f32(8 * 512).rearrange("p (m t) -> p m t", m=8) for _ in range(2)]
        osb = [bufs("o0", 8), bufs("o0", 8)]
        zbf = (A.bf16(8 * 512).rearrange("p (m t) -> p m t", m=8), bufs("zbf", 8))
        zsq = (A.bf16(8 * 512).rearrange("p (m t) -> p m t", m=8), bufs("zsq", 8))
        st = [A.f32(512) for _ in range(4)]
        NBL = NB if lim is None else lim

        def load(b):
            P.dma("sp", [(xs[b % 2], xT_v[:, :, b * 512:(b + 1) * 512])], [], xsb[b % 2], xsb[b % 2][0])
        load(0)
        prev = []
        for b in range(NBL):
            MD.pop(2)
            i = b % 2

            def store(b=b, i=i):
                P.dma("sp", [(X_v[:, :, b * 512:(b + 1) * 512], os_[i])], osb[i], [Xb[b]], osb[i][0])
            pre, post = ln_steps(xs[i], xsb[i], 512, zbf, zsq, st, lambda m: vec(m), lambda m: vec(8 + m), 0,
                                 os_[i], osb[i], 5, 6, [store])
            k = 0
            for k in range(17):
                pre[k]()
                if k < len(prev):
                    prev[k]()
            for p_ in prev[17:]:
                p_()
            if b + 1 < NBL:
                load(b + 1)
            for p_ in pre[17:]:
                p_()
            prev = post
        for p_ in prev:
            p_()
        MD.flush()

        def trm(e):
            ins = None
            for l in range(L):
                for j in range(48):
                    col = (l * 48 + j) * NSEQ
                    o = l * 6 * D + j * 128
                    ins = e.matmul(bank(0)[:, col:col + NSEQ], modrow[0:NSEQ, o:o + 128],
                                   ident_st[0:NSEQ, 0:NSEQ], start=True, stop=True)
            return ins
        P.op("pe", trm, mrb + [const_b], [psb[0]])
        P.op("dve", lambda e: e.tensor_tensor(
            out=modv.rearrange("p (a s) -> p a s", s=NSEQ),
            in0=bank(0)[:, 0:L * 48 * NSEQ].rearrange("p (a s) -> p a s", s=NSEQ),
            in1=bada.unsqueeze(2).to_broadcast([128, L * 48, NSEQ]), op=ALU.add), [psb[0], const_b], [mod_b])
        for l in range(L):
            for j0, mul in ((8, 1.0), (32, 1.0), (16, 1.0 / ALPHA), (40, 1.0 / ALPHA)):
                sl_ = modv4[:, l, j0:j0 + 8, :]
                P.op("dve", lambda e, sl_=sl_, mul=mul: e.tensor_scalar(
                    out=sl_, in0=sl_, scalar1=1.0, scalar2=mul, op0=ALU.add, op1=ALU.mult), [mod_b], [mod_b])

    def phaseA(l):
        A.off = PERSIST
        win = A.bf16(8 * 2048).rearrange("p (k n) -> p k n", k=8)
        winb = bufs("win", 4)
        for cg in (0, 1, 3, 2):
            P.dma("pool", [(win[:, :, cg * 512:(cg + 1) * 512], win_d[l, :, :, cg * 512:(cg + 1) * 512])],
                  [], [winb[cg]], winb[cg])
        xs = [A.f32(8 * 512).rearrange("p (m t) -> p m t", m=8) for _ in range(2)]
        xsb = [bufs("xa", 8), bufs("xa", 8)]
        hTs = [A.bf16(8 * 512).rearrange("p (m t) -> p m t", m=8) for _ in range(2)]
        hTbs = [bufs("hT", 8), bufs("hT", 8)]
        qo = [A.bf16(4 * 512).rearrange("p (m t) -> p m t", m=4) for _ in range(2)]
        ko = [A.bf16(4 * 512).rearrange("p (m t) -> p m t", m=4) for _ in range(2)]
        uo = [A.f32(4 * 512).rearrange("p (m t) -> p m t", m=4) for _ in range(2)]
        vo = [A.bf16(4 * 520).rearrange("p (n f) -> p n f", n=4) for _ in range(2)]
        qob = [bufs("qo", 4), bufs("qo", 4)]
        kob = [bufs("ko", 4), bufs("ko", 4)]
        uob = [bufs("uo", 4), bufs("uo", 4)]
        vob = [bufs("vo", 4), bufs("vo", 4)]
        for i in range(2):
            P.op("dve", lambda e, i=i: e.memset(vo[i], 1.0), [], vob[i])
        NBL = NB if lim is None else lim

        def load(b):
            P.dma("sp", [(xs[b % 2], X_v[:, :, b * 512:(b + 1) * 512])], [Xb[b]], xsb[b % 2], xsb[b % 2][0])

        def hcomp(b):
            s = b // 8
            x = xs[b % 2]
            hT = hTs[b % 2]
            for m in range(8):
                if m % 2 == 0:
                    P.op("act", lambda e, m=m: e.activation(out=hT[:, m, :], in_=x[:, m, :], func=AF.Identity,
                                                            bias=mv(l, m, s), scale=mv(l, 8 + m, s)),
                         [xsb[b % 2][m], mod_b], [hTbs[b % 2][m]])
                else:
                    P.op("dve", lambda e, m=m: e.tensor_scalar(out=hT[:, m, :], in0=x[:, m, :],
                                                               scalar1=mv(l, 8 + m, s), scalar2=mv(l, m, s),
                                                               op0=ALU.mult, op1=ALU.add),
                         [xsb[b % 2][m], mod_b], [hTbs[b % 2][m]])
        load(0)
        if NBL > 1:
            load(1)
        hcomp(0)
        pi = 0
        for b in range(NBL):
            hT = hTs[b % 2]
            hTb = hTbs[b % 2]
            sl = b % 2
            gi = 0
            for grp, (j0, dst, dstb) in enumerate(((0, qo[sl], qob[sl]), (4, ko[sl], kob[sl]), (12, uo[sl], uob[sl]))):
                for jj in range(4):
                    j = j0 + jj
                    pb = pi % 5
                    pi += 1

                    def mm(e, j=j, pb=pb):
                        ins = None
                        for k in range(8):
                            ins = e.matmul(bank(pb), win[:, k, j * 128:(j + 1) * 128], hT[:, k, :],
                                           start=(k == 0), stop=(k == 7))
                        return ins
                    P.op("pe", mm, [winb[j // 4]] + hTb, [psb[pb]])
                    if grp == 0:
                        P.op("act", lambda e, jj=jj, pb=pb, dst=dst: e.activation(
                            out=dst[:, jj, :], in_=bank(pb), func=AF.Identity, scale=0.125), [psb[pb]], [dstb[jj]])
                    elif grp == 1:
                        P.op("dve", lambda e, jj=jj, pb=pb, dst=dst: e.tensor_copy(out=dst[:, jj, :], in_=bank(pb)),
                             [psb[pb]], [dstb[jj]])
                    else:
                        P.op("act", lambda e, jj=jj, pb=pb, dst=dst: e.activation(
                            out=dst[:, jj, :], in_=bank(pb), func=AF.Identity), [psb[pb]], [dstb[jj]])
                    gi += 1
                    if gi == 4 and b + 1 < NBL:
                        hcomp(b + 1)
            for tt in range(4):
                pb = pi % 5
                pi += 1

                def mmv(e, tt=tt, pb=pb):
                    ins = None
                    for k in range(8):
                        ins = e.matmul(bank(pb), hT[:, k, tt * 128:(tt + 1) * 128], win[:, k, 1024:1536],
                                       start=(k == 0), stop=(k == 7))
                    return ins
                P.op("pe", mmv, [winb[2]] + hTb, [psb[pb]])
                P.op("dve", lambda e, tt=tt, pb=pb: e.tensor_copy(
                    out=vo[sl][:, tt, :].rearrange("p (h f) -> p h f", h=8)[:, :, 0:64],
                    in_=bank(pb).rearrange("p (h f) -> p h f", h=8)), [psb[pb]], [vob[sl][tt]])
            ts = slice(b * 512, (b + 1) * 512)
            P.dma("sp", [(Q_v[:, :, ts], qo[sl])], qob[sl], [Qb[b]], qob[sl][0])
            P.dma("sp", [(K_v[:, :, ts], ko[sl])], kob[sl], [Kb[b]], kob[sl][0])
            P.dma("sp", [(U_v[:, :, ts], uo[sl])], uob[sl], [Ub[b]], uob[sl][0])
            P.dma("sp", [(V_v[:, b * 4:(b + 1) * 4, :], vo[sl])], vob[sl], [Vb[b]], vob[sl][0])
            if b + 2 < NBL:
                load(b + 2)

    def phaseB(l):
        A.off = PERSIST
        wout = A.bf16(8 * 1024).rearrange("p (k n) -> p k n", k=8)
        woutb = P.buf("wout")
        P.dma("pool", [(wout[:, 4 * i:4 * i + 4, :], wout_d[l, :, 4 * i:4 * i + 4, :]) for i in range(2)],
              [], [woutb], woutb)
        wpool = A.bf16(4 * 128).rearrange("p (g n) -> p g n", g=4)
        P.dma("pool", [(wpool, wpool_d[l])], [], [woutb], woutb)
        tabs = [A.bf16(8192), A.bf16(8192)]
        tabb = P.buf("tab")
        tst = [A.f32(2048), A.f32(2048)]
        tstb = [P.buf("tst"), P.buf("tst")]
        it = 0
        for ti in range(2):
            for q4 in range(4):
                P.dma("sp", [(tst[it % 2], biasT_d[l, ti, :, q4 * 2048:(q4 + 1) * 2048])], [], [tstb[it % 2]],
                      tstb[it % 2])
                P.op("act", lambda e, ti=ti, q4=q4, it=it: e.activation(
                    out=tabs[ti][:, q4 * 2048:(q4 + 1) * 2048], in_=tst[it % 2], func=AF.Exp),
                    [tstb[it % 2]], [tabb])
                it += 1
        P.barrier()
        A.off -= 4096
        qs = [A.bf16(4 * 512).rearrange("p (m t) -> p m t", m=4) for _ in range(2)]
        ks = [A.bf16(4 * 1024).rearrange("p (m t) -> p m t", m=4) for _ in range(2)]
        vs = [A.bf16(8 * 520).rearrange("p (n f) -> p n f", n=8) for _ in range(2)]
        us = [A.f32(4 * 528).rearrange("p (g t) -> p g t", g=4) for _ in range(2)]
        xs1 = A.f32(8 * 512).rearrange("p (m t) -> p m t", m=8)
        qsb = [P.buf("qs"), P.buf("qs")]
        ksb = [P.buf("ks"), P.buf("ks")]
        vsb = [P.buf("vs"), P.buf("vs")]
        usb = [P.buf("us"), P.buf("us")]
        xsb1 = bufs("xb", 8)
        yT = A.bf16(8 * 512).rearrange("p (m t) -> p m t", m=8)
        yTa = P.buf("yTa")
        yTp = bufs("yTp", 4)
        pexp = [A.bf16(640) for _ in range(3)]
        pexpb = bufs("pexp", 3)
        pm = [A.bf16(640) for _ in range(3)]
        pmb = bufs("pm", 3)
        rc = A.f32(8)
        rcb = P.buf("rc")
        ytok = A.bf16(512)
        ytokb = P.buf("ytok")
        tA = A.f32(528)
        tB = A.f32(528)
        wsv = [A.f32(512) for _ in range(4)]
        wsb = bufs("ws", 4)
        t8 = A.f32(8)
        t8b = P.buf("t8")
        tmpb = P.buf("ptmp")
        mixed = A.bf16(4 * 512).rearrange("p (g t) -> p g t", g=4)
        mixb = bufs("mixed", 4)
        os1 = A.f32(8 * 512).rearrange("p (m t) -> p m t", m=8)
        osb1 = bufs("ob", 8)
        zbf = (A.bf16(8 * 512).rearrange("p (m t) -> p m t", m=8), bufs("zbf", 8))
        zsq = (A.bf16(8 * 512).rearrange("p (m t) -> p m t", m=8), bufs("zsq", 8))
        st = [A.f32(512) for _ in range(4)]
        Sb = [[psb[0], psb[1]], [psb[2], psb[3]]]
        Sap = [psA[:, 0:640], psA[:, 1024:1664]]
        Ob = [psb[4], psb[5]]
        Oap = [bank(4), bank(5)]
        invc4 = invc.rearrange("p (g s t) -> p g s t", g=4, s=2)
        NBL = NB if lim is None else lim - 1
        D = Deferred()

        def krange(b):
            r0 = (b % 8) * 8
            return max(0, r0 - 4), min(64, r0 + 12)

        def load(b):
            i = b % 2
            sq = b // 8
            r0 = (b % 8) * 8
            klo, khi = krange(b)
            t0 = sq * SEQ + klo * 64
            nt = (khi - klo) * 64
            kblocks = sorted(set([(t0) // 512, (t0 + nt - 1) // 512, b]))
            P.dma("sp", [(qs[i], Q_v[:, :, b * 512:(b + 1) * 512])], [Qb[b]], [qsb[i]], qsb[i])
            P.dma("sp", [(ks[i][:, :, 0:nt], K_v[:, :, t0:t0 + nt])], [Kb[kb] for kb in kblocks], [ksb[i]], ksb[i])
            P.dma("sp", [(vs[i][:, 0:nt // 128, :], V_v[:, t0 // 128:(t0 + nt) // 128, :])],
                  [Vb[kb] for kb in kblocks], [vsb[i]], vsb[i])
            tb0 = b * 512
            lo = tb0 - 8
            hi = tb0 + 520
            dlo = 0
            if r0 == 0:
                P.op("pool", lambda e, i=i: e.memset(us[i][:, :, 0:8], 0.0), [], [usb[i]])
                lo = tb0
                dlo = 8
            if r0 == 56:
                P.op("pool", lambda e, i=i: e.memset(us[i][:, :, 520:528], 0.0), [], [usb[i]])
                hi = tb0 + 512
            ublocks = sorted(set([lo // 512, (hi - 1) // 512, b]))
            P.dma("sp", [(us[i][:, :, dlo:dlo + hi - lo], U_v[:, :, lo:hi])], [Ub[kb] for kb in ublocks],
                  [usb[i]], usb[i])

        def loadx(b):
            P.dma("sp", [(xs1, X_v[:, :, b * 512:(b + 1) * 512])], [Xb[b]], xsb1, xsb1[0])

        load(0)
        loadx(0)
        item = 0
        gp = 0
        for b in range(NBL):
            i = b % 2
            sq = b // 8
            r0 = (b % 8) * 8
            klo, khi = krange(b)
            if b + 1 < NBL:
                load(b + 1)
            q_, k_, v_, u_ = qs[i], ks[i], vs[i], us[i]
            for g, w in enumerate((2, 4, 8, 16)):
                ug = u_[:, g, :]
                ws = wsv[g]
                if g == 0:
                    P.op("pool", lambda e, ug=ug, ws=ws: e.tensor_tensor(out=ws, in0=ug[:, 7:519], in1=ug[:, 8:520],
                                                                         op=ALU.add), [usb[i]], [wsb[g]])
                else:
                    P.op("pool", lambda e, ug=ug: e.tensor_tensor(out=tA[:, 0:527], in0=ug[:, 0:527], in1=ug[:, 1:528],
                                                                  op=ALU.add), [usb[i]], [tmpb])
                    if g == 1:
                        P.op("pool", lambda e, ws=ws: e.tensor_tensor(out=ws, in0=tA[:, 6:518], in1=tA[:, 8:520],
                                                                      op=ALU.add), [tmpb], [wsb[g]])
                    else:
                        P.op("pool", lambda e: e.tensor_tensor(out=tB[:, 0:525], in0=tA[:, 0:525], in1=tA[:, 2:527],
                                                               op=ALU.add), [tmpb], [tmpb])
                        if g == 2:
                            P.op("pool", lambda e, ws=ws: e.tensor_tensor(out=ws, in0=tB[:, 4:516], in1=tB[:, 8:520],
                                                                          op=ALU.add), [tmpb], [wsb[g]])
                        else:
                            P.op("pool", lambda e: e.tensor_tensor(out=tA[:, 0:521], in0=tB[:, 0:521],
                                                                   in1=tB[:, 4:525], op=ALU.add), [tmpb], [tmpb])
                            P.op("pool", lambda e, ws=ws: e.tensor_tensor(out=ws, in0=tA[:, 0:512], in1=tA[:, 8:520],
                                                                          op=ALU.add), [tmpb], [wsb[g]])
                P.op("dve", lambda e, g=g, w=w, ug=ug, ws=ws: e.scalar_tensor_tensor(
                    out=mixed[:, g, :], in0=ug[:, 8:520], scalar=-float(w), in1=ws, op0=ALU.mult, op1=ALU.add),
                    [wsb[g], usb[i]], [mixb[g]])
                for side, cond, c0 in ((0, r0 == 0, 0), (1, r0 == 56, 504)):
                    if cond:
                        P.op("dve", lambda e, g=g, side=side, c0=c0, ws=ws: e.tensor_tensor(
                            out=t8, in0=ws[:, c0:c0 + 8], in1=invc4[:, g, side, :], op=ALU.mult),
                            [wsb[g], const_b], [t8b])
                        P.op("dve", lambda e, g=g, c0=c0, ug=ug, w=w: e.scalar_tensor_tensor(
                            out=mixed[:, g, c0:c0 + 8], in0=ug[:, 8 + c0:16 + c0], scalar=-float(w), in1=t8,
                            op0=ALU.mult, op1=ALU.add), [t8b, usb[i]], [mixb[g]])
            items = []
            for pi_ in range(4):
                r = r0 + 2 * pi_
                if r < 4:
                    kr0s = [0, 2, 4, 6]
                    edge = 1
                elif r > 58:
                    kr0s = [56, 58, 60, 62]
                    edge = 1
                else:
                    kr0s = [r - 4, r - 2, r, r + 2, r + 4]
                    edge = 0
                for h in range(8):
                    items.append((pi_, r, kr0s, edge, h))

            def qk(idx):
                pi_, r, kr0s, edge, h = items[idx]
                n = len(kr0s)
                sslot = (item + idx) % 2
                hp = slice((h % 2) * 64, (h % 2) * 64 + 64)

                def f(e):
                    ins = None
                    for j, kr0 in enumerate(kr0s):
                        c = (n - 1 - j) * 128
                        ko_ = (kr0 - klo) * 64
                        ins = e.matmul(Sap[sslot][:, c:c + 128], k_[hp, h // 2, ko_:ko_ + 128],
                                       q_[hp, h // 2, pi_ * 128:(pi_ + 1) * 128], start=True, stop=True)
                    return ins
                P.op("pe", f, [ksb[i], qsb[i]], Sb[sslot])

            qk(0)
            qk(1)
            for idx in range(32):
                pi_, r, kr0s, edge, h = items[idx]
                n = len(kr0s)
                sslot = (item + idx) % 2
                ps3 = (item + idx) % 3
                P.op("act", lambda e, sslot=sslot, n=n, ps3=ps3: e.activation(
                    out=pexp[ps3][:, 0:n * 128], in_=Sap[sslot][:, 0:n * 128], func=AF.Exp),
                    Sb[sslot], [pexpb[ps3]])
                e_start = 7 - (kr0s[-1] - r)
                toff = h * 1024 + e_start * 64
                P.op("dve", lambda e, n=n, toff=toff, edge=edge, ps3=ps3: e.tensor_tensor(
                    out=pm[ps3][:, 0:n * 128], in0=pexp[ps3][:, 0:n * 128],
                    in1=tabs[edge][:, toff:toff + n * 128], op=ALU.mult),
                    [pexpb[ps3], tabb], [pmb[ps3]])
                if idx + 2 < 32:
                    qk(idx + 2)
                ob = h // 4
                oo = (h % 4) * 65

                def pv(e, kr0s=kr0s, n=n, ps3=ps3, ob=ob, oo=oo, h=h):
                    ins = None
                    for j, kr0 in enumerate(kr0s):
                        c = (n - 1 - j) * 128
                        ins = e.matmul(Oap[ob][:, oo:oo + 65], pm[ps3][:, c:c + 128],
                                       v_[:, (kr0 - klo) // 2, h * 65:(h + 1) * 65],
                                       start=(j == 0), stop=(j == n - 1))
                    return ins
                P.op("pe", pv, [pmb[ps3], vsb[i]], [Ob[ob]])
                if h == 7:
                    for ob2 in range(2):
                        o3 = Oap[ob2][:, 0:260].rearrange("p (h f) -> p h f", h=4)
                        P.op("dve", lambda e, o3=o3, ob2=ob2: e.reciprocal(
                            out=rc[:, ob2 * 4:ob2 * 4 + 4], in_=o3[:, :, 64]), [Ob[ob2]], [rcb])
                        P.op("dve", lambda e, o3=o3, ob2=ob2: e.tensor_tensor(
                            out=ytok[:, ob2 * 256:(ob2 + 1) * 256].rearrange("p (h f) -> p h f", h=4),
                            in0=o3[:, :, 0:64],
                            in1=rc[:, ob2 * 4:ob2 * 4 + 4].unsqueeze(2).to_broadcast([128, 4, 64]), op=ALU.mult),
                            [Ob[ob2], rcb], [ytokb])

                    def tr(e):
                        ins = None
                        for c4 in range(4):
                            ins = e.transpose(psT[:, c4 * 128:(c4 + 1) * 128], ytok[:, c4 * 128:(c4 + 1) * 128], ident_bf)
                        return ins
                    P.op("pe", tr, [ytokb, const_b], [pstb])
                    P.op("act", lambda e, pi_=pi_: e.activation(
                        out=yT[:, 0:4, pi_ * 128:(pi_ + 1) * 128],
                        in_=psT[:, 0:512].rearrange("p (c t) -> p c t", c=4), func=AF.Identity), [pstb], [yTa])
                D.pop(2)
            item += 32
            D.flush()
            for g in range(4):
                pb = 4 + gp % 3
                gp += 1
                P.op("pe", lambda e, g=g, pb=pb: e.matmul(bank(pb), wpool[:, g, :], mixed[:, g, :], start=True, stop=True),
                     [woutb, mixb[g]], [psb[pb]])
                P.op("act", lambda e, g=g, pb=pb: e.activation(out=yT[:, 4 + g, :], in_=bank(pb), func=AF.Identity,
                                                               scale=vec(48 + l * 4 + g)), [psb[pb], const_b], [yTp[g]])
            for m in range(8):
                pb = 4 + gp % 3
                gp += 1

                def mo(e, m=m, pb=pb):
                    ins = None
                    for k in range(8):
                        ins = e.matmul(bank(pb), wout[:, k, m * 128:(m + 1) * 128], yT[:, k, :],
                                       start=(k == 0), stop=(k == 7))
                    return ins
                P.op("pe", mo, [woutb, yTa] + yTp, [psb[pb]])
                P.op("dve", lambda e, m=m, pb=pb: e.scalar_tensor_tensor(
                    out=xs1[:, m, :], in0=bank(pb), scalar=mv(l, 16 + m, sq), in1=xs1[:, m, :],
                    op0=ALU.mult, op1=ALU.add), [psb[pb], xsb1[m], mod_b], [xsb1[m]])

            def store(b=b):
                P.dma("sp", [(X_v[:, :, b * 512:(b + 1) * 512], os1)], osb1, [Xb[b]], osb1[0])
            fin = [store]
            if b + 1 < NBL:
                fin.append(lambda b=b: loadx(b + 1))
            pre, post = ln_steps(xs1, xsb1, 512, zbf, zsq, st, lambda m: vec(16 + l * 8 + m),
                                 lambda m: vec(32 + l * 8 + m), 1, os1, osb1, 5, 6, fin)
            for s_ in pre + post:
                D.add(s_)
        D.flush()

    def phaseC(l, final):
        A.off = PERSIST
        NT = 256
        NBC = T // NT
        w1 = A.bf16(8 * 4096).rearrange("p (k n) -> p k n", k=8)
        w2 = A.bf16(32 * 1024).rearrange("p (k n) -> p k n", k=32)
        w1b = bufs("w1", 4)
        w2b = P.buf("w2")
        for cg in range(4):
            P.dma("pool", [(w1[:, :, cg * 1024:(cg + 1) * 1024], w1_d[l, :, :, cg * 1024:(cg + 1) * 1024])],
                  [], [w1b[cg]], w1b[cg])
        P.dma("pool", [(w2[:, 4 * i:4 * i + 4, :], w2_d[l, :, 4 * i:4 * i + 4, :]) for i in range(8)], [], [w2b], w2b)
        xs = [A.f32(8 * NT).rearrange("p (m t) -> p m t", m=8) for _ in range(3)]
        xsb = [bufs("xc", 8), bufs("xc", 8), bufs("xc", 8)]
        h2 = A.bf16(8 * NT).rearrange("p (m t) -> p m t", m=8)
        h2b = bufs("h2", 8)
        hid = A.bf16(32 * NT).rearrange("p (m t) -> p m t", m=32)
        hidb = bufs("hid", 32)
        rl = [A.f32(NT), A.f32(NT)]
        rlb = [P.buf("rl"), P.buf("rl")]
        os_ = [A.f32(8 * NT).rearrange("p (m t) -> p m t", m=8) for _ in range(2)]
        osb = [bufs("oc", 8), bufs("oc", 8)]
        zbf = (A.bf16(8 * NT).rearrange("p (m t) -> p m t", m=8), bufs("zbf", 8))
        zsq = (A.bf16(8 * NT).rearrange("p (m t) -> p m t", m=8), bufs("zsq", 8))
        st = [A.f32(NT) for _ in range(4)]
        dst_v = outT_v if final else X_v
        NBL = NBC if lim is None else 2 * (lim - 1)
        D = Deferred()

        def load(b):
            P.dma("sp", [(xs[b % 3], X_v[:, :, b * NT:(b + 1) * NT])], [Xb[(b * NT) // 512]], xsb[b % 3],
                  xsb[b % 3][0])

        def hcomp(b):
            sq = (b * NT) // SEQ
            x_ = xs[b % 3]
            for m in range(8):
                if m % 2 == 0:
                    P.op("act", lambda e, m=m: e.activation(out=h2[:, m, :], in_=x_[:, m, :], func=AF.Identity,
                                                            bias=mv(l, 24 + m, sq), scale=mv(l, 32 + m, sq)),
                         [xsb[b % 3][m], mod_b], [h2b[m]])
                else:
                    P.op("dve", lambda e, m=m: e.tensor_scalar(out=h2[:, m, :], in0=x_[:, m, :],
                                                               scalar1=mv(l, 32 + m, sq), scalar2=mv(l, 24 + m, sq),
                                                               op0=ALU.mult, op1=ALU.add),
                         [xsb[b % 3][m], mod_b], [h2b[m]])
        load(0)
        hcomp(0)
        pi = 0
        for b in range(NBL):
            i = b % 2
            sq = (b * NT) // SEQ
            if b + 1 < NBL:
                load(b + 1)
            x_ = xs[b % 3]
            xb_ = xsb[b % 3]
            for jc in range(32):
                pb = pi % 5
                pi += 1

                def m1(e, jc=jc, pb=pb):
                    ins = None
                    for k in range(8):
                        ins = e.matmul(bank(pb)[:, 0:NT], w1[:, k, jc * 128:(jc + 1) * 128], h2[:, k, :],
                                       start=(k == 0), stop=(k == 7))
                    return ins
                P.op("pe", m1, [w1b[jc // 8]] + h2b, [psb[pb]])
                ri = jc % 2
                P.op("act", lambda e, pb=pb, ri=ri: e.activation(out=rl[ri], in_=bank(pb)[:, 0:NT], func=AF.Relu),
                     [psb[pb]], [rlb[ri]])
                P.op("pool", lambda e, jc=jc, ri=ri: e.tensor_tensor(out=hid[:, jc, :], in0=rl[ri], in1=rl[ri],
                                                                     op=ALU.mult), [rlb[ri]], [hidb[jc]])
                D.pop(2)
            D.flush()
            if b + 1 < NBL:
                hcomp(b + 1)
            for m in range(8):
                pb = pi % 5
                pi += 1

                def m2(e, m=m, pb=pb):
                    ins = None
                    for k in range(32):
                        ins = e.matmul(bank(pb)[:, 0:NT], w2[:, k, m * 128:(m + 1) * 128], hid[:, k, :],
                                       start=(k == 0), stop=(k == 31))
                    return ins
                P.op("pe", m2, [w2b] + hidb, [psb[pb]])
                P.op("dve", lambda e, m=m, pb=pb: e.scalar_tensor_tensor(
                    out=x_[:, m, :], in0=bank(pb)[:, 0:NT], scalar=mv(l, 40 + m, sq), in1=x_[:, m, :],
                    op0=ALU.mult, op1=ALU.add), [psb[pb], xb_[m], mod_b], [xb_[m]])
            wr = [] if final else [Xb[(b * NT) // 512]]

            def store(b=b, i=i, wr=wr):
                P.dma("sp", [(dst_v[:, :, b * NT:(b + 1) * NT], os_[i])], osb[i], wr, osb[i][0])
            pre, post = ln_steps(x_, xb_, NT, zbf, zsq, st, lambda m: vec(56 + l * 8 + m),
                                 lambda m: vec(72 + l * 8 + m), 1, os_[i], osb[i], 5, 6, [store])
            for s_ in pre + post:
                D.add(s_)
        D.flush()

    phases = [("0", phase0)]
    for l in range(L):
        phases += [("A%d" % l, lambda l=l: phaseA(l)), ("B%d" % l, lambda l=l: phaseB(l)),
                   ("C%d" % l, lambda l=l: phaseC(l, l == L - 1))]
    for name, fn in phases:
        P.barrier()
        fn()
        if stop_after == name:
            break
    P.barrier()
    return nc


def _chunk_vec(v):
    return np.ascontiguousarray(v.reshape(-1, 128).T)


def _bias_tables(rpb_l):
    out = np.full((2, 128, 8, 16, 64), NEG, np.float32)
    c = np.arange(64)
    cs = np.clip(c - 8, 0, 48)
    for kl in range(2):
        for e in range(16):
            rel = 14 - e + kl
            if rel < 0 or rel > 14:
                continue
            for kc in range(64):
                valid = (kc >= cs) & (kc < cs + 16)
                idx = kc - c + 15
                cc = c[valid]
                vals = rpb_l[:, rel, idx[valid]]
                out[1, kl * 64 + kc, :, e, cc] = vals.T
                if 3 <= rel <= 10:
                    out[0, kl * 64 + kc, :, e, cc] = vals.T
    return out.reshape(2, 128, 8192)


def _prep_shared(inp):
    sh = {}
    sh["w_ada"] = np.ascontiguousarray(inp["w_ada"].reshape(L, 8, 128, 6 * D).transpose(0, 2, 1, 3))
    sh["b_ada"] = np.ascontiguousarray(inp["b_ada"].reshape(L, 48, 128).transpose(2, 0, 1))
    sh["w_in"] = np.ascontiguousarray(inp["w_in"].reshape(L, 8, 128, 2048).transpose(0, 2, 1, 3))
    sh["w_out"] = np.ascontiguousarray(inp["w_out"].reshape(L, 8, 128, 1024).transpose(0, 2, 1, 3))
    sh["w_mlp1"] = np.ascontiguousarray(inp["w_mlp1"].reshape(L, 8, 128, 4096).transpose(0, 2, 1, 3))
    sh["w_mlp2"] = np.ascontiguousarray(inp["w_mlp2"].reshape(L, 32, 128, 1024).transpose(0, 2, 1, 3))
    sh["w_pool"] = np.ascontiguousarray(inp["w_pool"].transpose(0, 2, 1, 3))
    vecs = np.zeros((128, 88), np.float32)
    vecs[:, 0:8] = _chunk_vec(inp["ln_in_g"])
    vecs[:, 8:16] = _chunk_vec(inp["ln_in_b"])
    for l in range(L):
        vecs[:, 16 + l * 8:24 + l * 8] = _chunk_vec(inp["ln1_g"][l])
        vecs[:, 32 + l * 8:40 + l * 8] = _chunk_vec(inp["ln1_b"][l])
        vecs[:, 48 + l * 4:52 + l * 4] = inp["pool_scale"][l].reshape(4, 128).T
        vecs[:, 56 + l * 8:64 + l * 8] = _chunk_vec(inp["ln2_g"][l])
        vecs[:, 72 + l * 8:80 + l * 8] = _chunk_vec(inp["ln2_b"][l])
    sh["vecs"] = vecs
    sh["biasT"] = np.stack([_bias_tables(np.asarray(inp["rpb"][l])) for l in range(L)])
    sh["ident"] = np.eye(128, dtype=np.float32)
    invc = np.zeros((4, 2, 8), np.float32)
    for g, w in enumerate((2, 4, 8, 16)):
        for side, toks in ((0, np.arange(0, 8)), (1, np.arange(SEQ - 8, SEQ))):
            lo = np.clip(toks - w // 2, 0, SEQ)
            hi = np.clip(toks - w // 2 + w, 0, SEQ)
            invc[g, side] = w / (hi - lo)
    sh["invc"] = np.ascontiguousarray(np.broadcast_to(invc.reshape(1, 64), (128, 64)))
    return sh


VEC_COLS = 88


def _core_inputs(inp, sh, core):
    b0 = core * NSEQ
    x2 = np.asarray(inp["x"][b0:b0 + NSEQ]).reshape(T, D)
    m = dict(sh)
    m["xT"] = np.ascontiguousarray(x2.T)
    c2 = np.asarray(inp["c"][b0:b0 + NSEQ])
    m["cT"] = np.ascontiguousarray(c2.reshape(NSEQ, 8, 128).transpose(2, 1, 0))
    return m


_NC_CACHE = {}


def kernel(**inputs):
    inp = {k: np.asarray(v) for k, v in inputs.items()}
    sh = _prep_shared(inp)
    if "nc" not in _NC_CACHE:
        _NC_CACHE["nc"] = build_nc()
    nc = _NC_CACHE["nc"]
    in_maps = [_core_inputs(inp, sh, c) for c in range(NCORES)]
    res = run_bass_kernel_spmd(nc, in_maps, core_ids=list(range(NCORES)))
    out = np.empty((NCORES * NSEQ, SEQ, D), np.float32)
    for c in range(NCORES):
        oT = np.asarray(res.results[c]["outT"])
        out[c * NSEQ:(c + 1) * NSEQ] = oT.T.reshape(NSEQ, SEQ, D)
    return out
```

```python
import numpy as np
import concourse.bass as bass
import concourse.mybir as mybir
from concourse.bass_utils import run_bass_kernel_spmd

F32 = mybir.dt.float32
BF16 = mybir.dt.bfloat16
AF = mybir.ActivationFunctionType
ALU = mybir.AluOpType

NCORES = 8
D = 1024
SEQ = 4096
NSEQ = 2
T = NSEQ * SEQ
L = 2
ALPHA = (2.0 * L) ** 0.25
EPS_P = 1e-5 / (ALPHA * ALPHA)
NEG = -80.0
SBUF_WORDS = 52500


class Buf:
    __slots__ = ("name", "w", "r", "dsem", "dcnt")

    def __init__(self, name):
        self.name = name
        self.w = None
        self.r = {}
        self.dsem = None
        self.dcnt = 0


class Prog:
    def __init__(self, nc):
        self.nc = nc
        self.eng = {"pe": nc.tensor, "act": nc.scalar, "dve": nc.vector, "pool": nc.gpsimd, "sp": nc.sync}
        self.semobj = {}
        self.cnt = {}
        for k in ("pe", "act", "dve", "pool"):
            self.semobj[k] = nc.alloc_semaphore("s_" + k)
            self.cnt[k] = 0
        self.seen = {k: {} for k in self.eng}
        self.dbufs = []
        self.nbuf = 0

    def buf(self, name):
        self.nbuf += 1
        return Buf("%s_%d" % (name, self.nbuf))

    def _deps(self, e, reads, writes):
        need = {}

        def add(k, v):
            if e == "pe" and k == "pe":
                return
            if need.get(k, 0) < v:
                need[k] = v

        for b in reads:
            if b.w is not None:
                add(*b.w)
        for b in writes:
            if b.w is not None:
                add(*b.w)
            for k, v in b.r.items():
                add(k, v)
        seen = self.seen[e]
        for k, v in need.items():
            if seen.get(k, 0) >= v:
                continue
            self.eng[e].wait_ge(self.semobj[k], v)
            seen[k] = v

    def _mark(self, ev, reads, writes):
        k, v = ev
        for b in reads:
            if b.r.get(k, 0) < v:
                b.r[k] = v
        for b in writes:
            b.w = ev
            b.r = {}

    def op(self, e, fn, reads=(), writes=()):
        self._deps(e, reads, writes)
        ins = fn(self.eng[e])
        self.cnt[e] += 1
        ins.then_inc(self.semobj[e], 1)
        self._mark((e, self.cnt[e]), reads, writes)

    def dma(self, q, pairs, reads, writes, sembuf):
        self._deps(q, reads, writes)
        if sembuf.dsem is None:
            sembuf.dsem = self.nc.alloc_semaphore("d_" + sembuf.name)
            self.semobj["d:" + sembuf.name] = sembuf.dsem
            self.dbufs.append(sembuf)
        for o, i in pairs:
            ins = self.eng[q].dma_start(out=o, in_=i)
            sembuf.dcnt += 16
            ins.then_inc(sembuf.dsem, 16)
        self._mark(("d:" + sembuf.name, sembuf.dcnt), reads, writes)

    def barrier(self, engines=("pe", "act", "dve", "pool", "sp")):
        evs = [(k, self.cnt[k]) for k in ("pe", "act", "dve", "pool") if self.cnt[k] > 0]
        evs += [("d:" + b.name, b.dcnt) for b in self.dbufs if b.dcnt > 0]
        for e in engines:
            seen = self.seen[e]
            for k, v in evs:
                if seen.get(k, 0) >= v:
                    continue
                self.eng[e].wait_ge(self.semobj[k], v)
                seen[k] = v


class Arena:
    def __init__(self, big):
        self.big = big
        self.off = 0

    def f32(self, n):
        a = self.big[:, self.off:self.off + n]
        self.off += n
        assert self.off <= SBUF_WORDS, self.off
        return a

    def bf16(self, n):
        w = (n + 1) // 2
        a = self.big[:, self.off:self.off + w].bitcast(BF16)
        self.off += w
        assert self.off <= SBUF_WORDS, self.off
        return a


def build_nc(debug=False, stop_after=None, lim=None):
    nc = bass.Bass("TRN2", target_bir_lowering=False)
    P = Prog(nc)

    def din(name, shape, dt=F32):
        return nc.dram_tensor(name, list(shape), dt, kind="ExternalInput").ap()

    xT_d = din("xT", [D, T])
    cT_d = din("cT", [128, 8, NSEQ])
    wada_d = din("w_ada", [L, 128, 8, 6 * D])
    bada_d = din("b_ada", [128, L, 48])
    win_d = din("w_in", [L, 128, 8, 2048])
    wout_d = din("w_out", [L, 128, 8, 1024])
    w1_d = din("w_mlp1", [L, 128, 8, 4096])
    w2_d = din("w_mlp2", [L, 128, 32, 1024])
    wpool_d = din("w_pool", [L, 128, 4, 128])
    vecs_d = din("vecs", [128, 88])
    biasT_d = din("biasT", [L, 2, 128, 8192])
    ident_d = din("ident", [128, 128])
    invc_d = din("invc", [128, 64])
    outT_d = nc.dram_tensor("outT", [D, T], F32, kind="ExternalOutput").ap()
    skind = "ExternalOutput" if debug else "Internal"
    X_d = nc.dram_tensor("Xs", [D, T], F32, kind=skind).ap()
    Q_d = nc.dram_tensor("Qs", [512, T], BF16, kind=skind).ap()
    K_d = nc.dram_tensor("Ks", [512, T], BF16, kind=skind).ap()
    U_d = nc.dram_tensor("Us", [512, T], F32, kind=skind).ap()
    V_d = nc.dram_tensor("Vs", [T, 520], BF16, kind=skind).ap()

    xT_v = xT_d.rearrange("(m p) t -> p m t", p=128)
    outT_v = outT_d.rearrange("(m p) t -> p m t", p=128)
    X_v = X_d.rearrange("(m p) t -> p m t", p=128)
    Q_v = Q_d.rearrange("(m p) t -> p m t", p=128)
    K_v = K_d.rearrange("(m p) t -> p m t", p=128)
    U_v = U_d.rearrange("(m p) t -> p m t", p=128)
    V_v = V_d.rearrange("(n p) f -> p n f", p=128)

    big = nc.alloc_sbuf_tensor("big", [128, SBUF_WORDS], F32)
    psA = nc.alloc_psum_tensor("psA", [128, 3584], F32)
    psT = nc.alloc_psum_tensor("psT", [128, 1024], BF16)
    A = Arena(big)

    NB = T // 512
    Xb = [P.buf("Xd") for _ in range(NB)]
    Qb = [P.buf("Qd") for _ in range(NB)]
    Kb = [P.buf("Kd") for _ in range(NB)]
    Ub = [P.buf("Ud") for _ in range(NB)]
    Vb = [P.buf("Vd") for _ in range(NB)]

    psb = [P.buf("ps") for _ in range(7)]
    pstb = P.buf("pst")

    def bank(i, n=512):
        return psA[:, i * 512:i * 512 + n]

    ones_bf = A.bf16(128)
    ident_bf = A.bf16(128)
    vecs = A.f32(88)
    invc = A.f32(64)
    modv = A.f32(L * 48 * NSEQ)
    cact = A.f32(8 * NSEQ)
    bada = A.f32(L * 48)
    const_b = P.buf("const")
    mod_b = P.buf("mod")
    epsv = A.f32(2)
    PERSIST = A.off

    modv4 = modv.rearrange("p (l j s) -> p l j s", l=L, j=48)

    def mv(l, j, s):
        return modv4[:, l, j, s:s + 1]

    def vec(i):
        return vecs[:, i:i + 1]
    P.dma("sp", [(vecs, vecs_d[:, :]), (invc, invc_d[:, :]), (cact, cT_d.rearrange("p k s -> p (k s)")),
                 (bada, bada_d.rearrange("p l j -> p (l j)"))], [], [const_b], const_b)
    ident_st = A.f32(128)
    P.dma("sp", [(ident_st, ident_d[:, :])], [], [const_b], const_b)
    P.op("dve", lambda e: e.tensor_copy(out=ident_bf, in_=ident_st), [const_b], [const_b])
    P.op("dve", lambda e: e.memset(ones_bf, 1.0 / D), [], [const_b])
    P.op("act", lambda e: e.activation(out=cact, in_=cact, func=AF.Silu), [const_b], [const_b])

    class Deferred:
        def __init__(self):
            self.q = []

        def add(self, fn):
            self.q.append(fn)

        def pop(self, n=1):
            for _ in range(n):
                if self.q:
                    self.q.pop(0)()

        def flush(self):
            while self.q:
                self.q.pop(0)()

    def bufs(name, n):
        return [P.buf(name) for _ in range(n)]

    P.op("dve", lambda e: e.memset(epsv[:, 0:1], 1e-5), [], [const_b])
    P.op("dve", lambda e: e.memset(epsv[:, 1:2], EPS_P), [], [const_b])
    for l in range(L):
        for g, w in enumerate((2, 4, 8, 16)):
            c = 48 + l * 4 + g
            P.op("dve", lambda e, c=c, w=w: e.tensor_scalar(out=vecs[:, c:c + 1], in0=vecs[:, c:c + 1],
                                                            scalar1=1.0 / w, scalar2=None, op0=ALU.mult),
                 [const_b], [const_b])

    def eps_ap(eps):
        return epsv[:, 0:1] if eps == 0 else epsv[:, 1:2]

    st_b = [P.buf("st") for _ in range(4)]

    def ln_steps(z, zb_, N, zbf, zsq, st, gam, bet, eps, out, outb, ps1, ps2, final_steps=()):
        msq, sd, Aa, Bb = st
        stb = st_b
        pre = []
        for m in range(8):
            pre.append(lambda m=m: P.op("act", lambda e: e.activation(
                out=zsq[0][:, m, :], in_=z[:, m, :], func=AF.Square), [zb_[m]], [zsq[1][m]]))
            pre.append(lambda m=m: P.op("dve", lambda e: e.tensor_copy(out=zbf[0][:, m, :], in_=z[:, m, :]),
                                        [zb_[m]], [zbf[1][m]]))

        def l2a():
            def s1(e):
                ins = None
                for m in range(8):
                    ins = e.matmul(bank(ps1)[:, 0:N], ones_bf, zbf[0][:, m, :], start=(m == 0), stop=(m == 7))
                return ins

            def s2(e):
                ins = None
                for m in range(8):
                    ins = e.matmul(bank(ps2)[:, 0:N], ones_bf, zsq[0][:, m, :], start=(m == 0), stop=(m == 7))
                return ins
            P.op("pe", s1, zbf[1] + [const_b], [psb[ps1]])
            P.op("pe", s2, zsq[1] + [const_b], [psb[ps2]])
        pre.append(l2a)
        pre.append(lambda: P.op("act", lambda e: e.activation(out=msq, in_=bank(ps1)[:, 0:N], func=AF.Square),
                                [psb[ps1]], [stb[0]]))
        pre.append(lambda: P.op("dve", lambda e: e.tensor_tensor(out=sd, in0=bank(ps2)[:, 0:N], in1=msq,
                                                                 op=ALU.subtract), [psb[ps2], stb[0]], [stb[1]]))
        pre.append(lambda: P.op("act", lambda e: e.activation(out=sd, in_=sd, func=AF.Ln, bias=eps_ap(eps), scale=1.0),
                                [stb[1], const_b], [stb[1]]))
        pre.append(lambda: P.op("act", lambda e: e.activation(out=Aa, in_=sd, func=AF.Exp, scale=-0.5),
                                [stb[1]], [stb[2]]))
        pre.append(lambda: P.op("dve", lambda e: e.scalar_tensor_tensor(
            out=Bb, in0=bank(ps1)[:, 0:N], scalar=-1.0, in1=Aa, op0=ALU.mult, op1=ALU.mult),
            [psb[ps1], stb[2]], [stb[3]]))
        post = []

        def zA(m):
            P.op("dve", lambda e: e.tensor_tensor(out=z[:, m, :], in0=z[:, m, :], in1=Aa, op=ALU.mult),
                 [zb_[m], stb[2]], [zb_[m]])

        def zB(m):
            P.op("pool", lambda e: e.tensor_tensor(out=z[:, m, :], in0=z[:, m, :], in1=Bb, op=ALU.add),
                 [zb_[m], stb[3]], [zb_[m]])

        def aff(m):
            P.op("act", lambda e: e.activation(out=out[:, m, :], in_=z[:, m, :], func=AF.Identity,
                                               bias=bet(m), scale=gam(m)), [zb_[m], const_b], [outb[m]])
        for t in range(12):
            if t < 8:
                post.append(lambda m=t: zA(m))
            if 2 <= t < 10:
                post.append(lambda m=t - 2: zB(m))
            if 4 <= t < 12:
                post.append(lambda m=t - 4: aff(m))
        post.extend(final_steps)
        return pre, post

    def phase0():
        A.off = PERSIST
        wsl = [A.bf16(8 * 512) for _ in range(3)]
        wslb = bufs("wsl", 3)
        cact_bf = A.bf16(8 * NSEQ)
        P.op("dve", lambda e: e.tensor_copy(out=cact_bf, in_=cact), [const_b], [const_b])
        cact3 = cact_bf.rearrange("p (k s) -> p k s", k=8)
        MD = Deferred()

        def mstep(l, sl, it):
            wb3 = wsl[it % 3].rearrange("p (k n) -> p k n", k=8)
            wbb = wslb[it % 3]
            P.dma("pool", [(wb3, wada_d[l, :, :, sl * 512:(sl + 1) * 512])], [], [wbb], wbb)
            for jj in range(4):
                j = sl * 4 + jj
                col = (l * 48 + j) * NSEQ

                def mm(e, jj=jj, col=col):
                    ins = None
                    for k in range(8):
                        ins = e.matmul(bank(0)[:, col:col + NSEQ], wb3[:, k, jj * 128:(jj + 1) * 128],
                                       cact3[:, k, :], start=(k == 0), stop=(k == 7))
                    return ins
                P.op("pe", mm, [wbb, const_b], [psb[0]])
        it = 0
        for l in range(L):
            for sl in range(12):
                MD.add(lambda l=l, sl=sl, it=it: mstep(l, sl, it))
                it += 1

        xs = [A.f32(8 * 512).rearrange("p (m t) -> p m t", m=8) for _ in range(2)]
        xsb = [bufs("x0", 8), bufs("x0", 8)]
        os_ = [A.f32(8 * 512).rearrange("p (m t) -> p m t", m=8) for _ in range(2)]
        osb = [bufs("o0", 8), bufs("o0", 8)]
        zbf = (A.bf16(8 * 512).rearrange("p (m t) -> p m t", m=8), bufs("zbf", 8))
        zsq = (A.bf16(8 * 512).rearrange("p (m t) -> p m t", m=8), bufs("zsq", 8))
        st = [A.f32(512) for _ in range(4)]
        NBL = NB if lim is None else lim

        def load(b):
            P.dma("sp", [(xs[b % 2], xT_v[:, :, b * 512:(b + 1) * 512])], [], xsb[b % 2], xsb[b % 2][0])
        load(0)
        prev = []
        for b in range(NBL):
            MD.pop(2)
            i = b % 2

            def store(b=b, i=i):
                P.dma("sp", [(X_v[:, :, b * 512:(b + 1) * 512], os_[i])], osb[i], [Xb[b]], osb[i][0])
            pre, post = ln_steps(xs[i], xsb[i], 512, zbf, zsq, st, lambda m: vec(m), lambda m: vec(8 + m), 0,
                                 os_[i], osb[i], 5, 6, [store])
            k = 0
            for k in range(17):
                pre[k]()
                if k < len(prev):
                    prev[k]()
            for p_ in prev[17:]:
                p_()
            if b + 1 < NBL:
                load(b + 1)
            for p_ in pre[17:]:
                p_()
            prev = post
        for p_ in prev:
            p_()
        MD.flush()
        P.op("dve", lambda e: e.tensor_tensor(
            out=modv.rearrange("p (a s) -> p a s", s=NSEQ),
            in0=bank(0)[:, 0:L * 48 * NSEQ].rearrange("p (a s) -> p a s", s=NSEQ),
            in1=bada.unsqueeze(2).to_broadcast([128, L * 48, NSEQ]), op=ALU.add), [psb[0], const_b], [mod_b])
        for l in range(L):
            for j0, mul in ((8, 1.0), (32, 1.0), (16, 1.0 / ALPHA), (40, 1.0 / ALPHA)):
                sl_ = modv4[:, l, j0:j0 + 8, :]
                P.op("dve", lambda e, sl_=sl_, mul=mul: e.tensor_scalar(
                    out=sl_, in0=sl_, scalar1=1.0, scalar2=mul, op0=ALU.add, op1=ALU.mult), [mod_b], [mod_b])

    def phaseA(l):
        A.off = PERSIST
        win = A.bf16(8 * 2048).rearrange("p (k n) -> p k n", k=8)
        winb = P.buf("win")
        P.dma("pool", [(win[:, 2 * i:2 * i + 2, :], win_d[l, :, 2 * i:2 * i + 2, :]) for i in range(4)],
              [], [winb], winb)
        xs = [A.f32(8 * 512).rearrange("p (m t) -> p m t", m=8) for _ in range(2)]
        xsb = [bufs("xa", 8), bufs("xa", 8)]
        hTs = [A.bf16(8 * 512).rearrange("p (m t) -> p m t", m=8) for _ in range(2)]
        hTbs = [bufs("hT", 8), bufs("hT", 8)]
        qo = [A.bf16(4 * 512).rearrange("p (m t) -> p m t", m=4) for _ in range(2)]
        ko = [A.bf16(4 * 512).rearrange("p (m t) -> p m t", m=4) for _ in range(2)]
        uo = [A.f32(4 * 512).rearrange("p (m t) -> p m t", m=4) for _ in range(2)]
        vo = [A.bf16(4 * 520).rearrange("p (n f) -> p n f", n=4) for _ in range(2)]
        qob = [bufs("qo", 4), bufs("qo", 4)]
        kob = [bufs("ko", 4), bufs("ko", 4)]
        uob = [bufs("uo", 4), bufs("uo", 4)]
        vob = [bufs("vo", 4), bufs("vo", 4)]
        for i in range(2):
            P.op("dve", lambda e, i=i: e.memset(vo[i], 1.0), [], vob[i])
        NBL = NB if lim is None else lim

        def load(b):
            P.dma("sp", [(xs[b % 2], X_v[:, :, b * 512:(b + 1) * 512])], [Xb[b]], xsb[b % 2], xsb[b % 2][0])

        def hcomp(b):
            s = b // 8
            x = xs[b % 2]
            hT = hTs[b % 2]
            for m in range(8):
                if m % 2 == 0:
                    P.op("act", lambda e, m=m: e.activation(out=hT[:, m, :], in_=x[:, m, :], func=AF.Identity,
                                                            bias=mv(l, m, s), scale=mv(l, 8 + m, s)),
                         [xsb[b % 2][m], mod_b], [hTbs[b % 2][m]])
                else:
                    P.op("dve", lambda e, m=m: e.tensor_scalar(out=hT[:, m, :], in0=x[:, m, :],
                                                               scalar1=mv(l, 8 + m, s), scalar2=mv(l, m, s),
                                                               op0=ALU.mult, op1=ALU.add),
                         [xsb[b % 2][m], mod_b], [hTbs[b % 2][m]])
        load(0)
        if NBL > 1:
            load(1)
        hcomp(0)
        pi = 0
        for b in range(NBL):
            hT = hTs[b % 2]
            hTb = hTbs[b % 2]
            sl = b % 2
            gi = 0
            for grp, (j0, dst, dstb) in enumerate(((0, qo[sl], qob[sl]), (4, ko[sl], kob[sl]), (12, uo[sl], uob[sl]))):
                for jj in range(4):
                    j = j0 + jj
                    pb = pi % 5
                    pi += 1

                    def mm(e, j=j, pb=pb):
                        ins = None
                        for k in range(8):
                            ins = e.matmul(bank(pb), win[:, k, j * 128:(j + 1) * 128], hT[:, k, :],
                                           start=(k == 0), stop=(k == 7))
                        return ins
                    P.op("pe", mm, [winb] + hTb, [psb[pb]])
                    if grp == 0:
                        P.op("act", lambda e, jj=jj, pb=pb, dst=dst: e.activation(
                            out=dst[:, jj, :], in_=bank(pb), func=AF.Identity, scale=0.125), [psb[pb]], [dstb[jj]])
                    elif grp == 1:
                        P.op("dve", lambda e, jj=jj, pb=pb, dst=dst: e.tensor_copy(out=dst[:, jj, :], in_=bank(pb)),
                             [psb[pb]], [dstb[jj]])
                    else:
                        P.op("act", lambda e, jj=jj, pb=pb, dst=dst: e.activation(
                            out=dst[:, jj, :], in_=bank(pb), func=AF.Identity), [psb[pb]], [dstb[jj]])
                    gi += 1
                    if gi == 4 and b + 1 < NBL:
                        hcomp(b + 1)
            for tt in range(4):
                pb = pi % 5
                pi += 1

                def mmv(e, tt=tt, pb=pb):
                    ins = None
                    for k in range(8):
                        ins = e.matmul(bank(pb), hT[:, k, tt * 128:(tt + 1) * 128], win[:, k, 1024:1536],
                                       start=(k == 0), stop=(k == 7))
                    return ins
                P.op("pe", mmv, [winb] + hTb, [psb[pb]])
                P.op("dve", lambda e, tt=tt, pb=pb: e.tensor_copy(
                    out=vo[sl][:, tt, :].rearrange("p (h f) -> p h f", h=8)[:, :, 0:64],
                    in_=bank(pb).rearrange("p (h f) -> p h f", h=8)), [psb[pb]], [vob[sl][tt]])
            ts = slice(b * 512, (b + 1) * 512)
            P.dma("sp", [(Q_v[:, :, ts], qo[sl])], qob[sl], [Qb[b]], qob[sl][0])
            P.dma("sp", [(K_v[:, :, ts], ko[sl])], kob[sl], [Kb[b]], kob[sl][0])
            P.dma("sp", [(U_v[:, :, ts], uo[sl])], uob[sl], [Ub[b]], uob[sl][0])
            P.dma("sp", [(V_v[:, b * 4:(b + 1) * 4, :], vo[sl])], vob[sl], [Vb[b]], vob[sl][0])
            if b + 2 < NBL:
                load(b + 2)

    def phaseB(l):
        A.off = PERSIST
        wout = A.bf16(8 * 1024).rearrange("p (k n) -> p k n", k=8)
        woutb = P.buf("wout")
        P.dma("pool", [(wout[:, 4 * i:4 * i + 4, :], wout_d[l, :, 4 * i:4 * i + 4, :]) for i in range(2)],
              [], [woutb], woutb)
        wpool = A.bf16(4 * 128).rearrange("p (g n) -> p g n", g=4)
        P.dma("pool", [(wpool, wpool_d[l])], [], [woutb], woutb)
        tabs = [A.bf16(8192), A.bf16(8192)]
        tabb = P.buf("tab")
        tst = [A.f32(2048), A.f32(2048)]
        tstb = [P.buf("tst"), P.buf("tst")]
        it = 0
        for ti in range(2):
            for q4 in range(4):
                P.dma("sp", [(tst[it % 2], biasT_d[l, ti, :, q4 * 2048:(q4 + 1) * 2048])], [], [tstb[it % 2]],
                      tstb[it % 2])
                P.op("act", lambda e, ti=ti, q4=q4, it=it: e.activation(
                    out=tabs[ti][:, q4 * 2048:(q4 + 1) * 2048], in_=tst[it % 2], func=AF.Exp),
                    [tstb[it % 2]], [tabb])
                it += 1
        P.barrier()
        A.off -= 4096
        qs = [A.bf16(4 * 512).rearrange("p (m t) -> p m t", m=4) for _ in range(2)]
        ks = [A.bf16(4 * 1024).rearrange("p (m t) -> p m t", m=4) for _ in range(2)]
        vs = [A.bf16(8 * 520).rearrange("p (n f) -> p n f", n=8) for _ in range(2)]
        us = [A.f32(4 * 528).rearrange("p (g t) -> p g t", g=4) for _ in range(2)]
        xs1 = A.f32(8 * 512).rearrange("p (m t) -> p m t", m=8)
        qsb = [P.buf("qs"), P.buf("qs")]
        ksb = [P.buf("ks"), P.buf("ks")]
        vsb = [P.buf("vs"), P.buf("vs")]
        usb = [P.buf("us"), P.buf("us")]
        xsb1 = bufs("xb", 8)
        yT = A.bf16(8 * 512).rearrange("p (m t) -> p m t", m=8)
        yTa = P.buf("yTa")
        yTp = bufs("yTp", 4)
        pexp = [A.bf16(640) for _ in range(3)]
        pexpb = bufs("pexp", 3)
        pm = [A.bf16(640) for _ in range(3)]
        pmb = bufs("pm", 3)
        rc = A.f32(8)
        rcb = P.buf("rc")
        ytok = A.bf16(512)
        ytokb = P.buf("ytok")
        tA = A.f32(528)
        tB = A.f32(528)
        wsv = [A.f32(512) for _ in range(4)]
        wsb = bufs("ws", 4)
        t8 = A.f32(8)
        t8b = P.buf("t8")
        tmpb = P.buf("ptmp")
        mixed = A.bf16(4 * 512).rearrange("p (g t) -> p g t", g=4)
        mixb = bufs("mixed", 4)
        os1 = A.f32(8 * 512).rearrange("p (m t) -> p m t", m=8)
        osb1 = bufs("ob", 8)
        zbf = (A.bf16(8 * 512).rearrange("p (m t) -> p m t", m=8), bufs("zbf", 8))
        zsq = (A.bf16(8 * 512).rearrange("p (m t) -> p m t", m=8), bufs("zsq", 8))
        st = [A.f32(512) for _ in range(4)]
        Sb = [[psb[0], psb[1]], [psb[2], psb[3]]]
        Sap = [psA[:, 0:640], psA[:, 1024:1664]]
        Ob = [psb[4], psb[5]]
        Oap = [bank(4), bank(5)]
        invc4 = invc.rearrange("p (g s t) -> p g s t", g=4, s=2)
        NBL = NB if lim is None else lim - 1
        D = Deferred()

        def krange(b):
            r0 = (b % 8) * 8
            return max(0, r0 - 4), min(64, r0 + 12)

        def load(b):
            i = b % 2
            sq = b // 8
            r0 = (b % 8) * 8
            klo, khi = krange(b)
            t0 = sq * SEQ + klo * 64
            nt = (khi - klo) * 64
            kblocks = sorted(set([(t0) // 512, (t0 + nt - 1) // 512, b]))
            P.dma("sp", [(qs[i], Q_v[:, :, b * 512:(b + 1) * 512])], [Qb[b]], [qsb[i]], qsb[i])
            P.dma("sp", [(ks[i][:, :, 0:nt], K_v[:, :, t0:t0 + nt])], [Kb[kb] for kb in kblocks], [ksb[i]], ksb[i])
            P.dma("sp", [(vs[i][:, 0:nt // 128, :], V_v[:, t0 // 128:(t0 + nt) // 128, :])],
                  [Vb[kb] for kb in kblocks], [vsb[i]], vsb[i])
            tb0 = b * 512
            lo = tb0 - 8
            hi = tb0 + 520
            dlo = 0
            if r0 == 0:
                P.op("pool", lambda e, i=i: e.memset(us[i][:, :, 0:8], 0.0), [], [usb[i]])
                lo = tb0
                dlo = 8
            if r0 == 56:
                P.op("pool", lambda e, i=i: e.memset(us[i][:, :, 520:528], 0.0), [], [usb[i]])
                hi = tb0 + 512
            ublocks = sorted(set([lo // 512, (hi - 1) // 512, b]))
            P.dma("sp", [(us[i][:, :, dlo:dlo + hi - lo], U_v[:, :, lo:hi])], [Ub[kb] for kb in ublocks],
                  [usb[i]], usb[i])

        def loadx(b):
            P.dma("sp", [(xs1, X_v[:, :, b * 512:(b + 1) * 512])], [Xb[b]], xsb1, xsb1[0])

        load(0)
        loadx(0)
        item = 0
        gp = 0
        for b in range(NBL):
            i = b % 2
            sq = b // 8
            r0 = (b % 8) * 8
            klo, khi = krange(b)
            if b + 1 < NBL:
                load(b + 1)
            q_, k_, v_, u_ = qs[i], ks[i], vs[i], us[i]
            for g, w in enumerate((2, 4, 8, 16)):
                ug = u_[:, g, :]
                ws = wsv[g]
                if g == 0:
                    P.op("pool", lambda e, ug=ug, ws=ws: e.tensor_tensor(out=ws, in0=ug[:, 7:519], in1=ug[:, 8:520],
                                                                         op=ALU.add), [usb[i]], [wsb[g]])
                else:
                    P.op("pool", lambda e, ug=ug: e.tensor_tensor(out=tA[:, 0:527], in0=ug[:, 0:527], in1=ug[:, 1:528],
                                                                  op=ALU.add), [usb[i]], [tmpb])
                    if g == 1:
                        P.op("pool", lambda e, ws=ws: e.tensor_tensor(out=ws, in0=tA[:, 6:518], in1=tA[:, 8:520],
                                                                      op=ALU.add), [tmpb], [wsb[g]])
                    else:
                        P.op("pool", lambda e: e.tensor_tensor(out=tB[:, 0:525], in0=tA[:, 0:525], in1=tA[:, 2:527],
                                                               op=ALU.add), [tmpb], [tmpb])
                        if g == 2:
                            P.op("pool", lambda e, ws=ws: e.tensor_tensor(out=ws, in0=tB[:, 4:516], in1=tB[:, 8:520],
                                                                          op=ALU.add), [tmpb], [wsb[g]])
                        else:
                            P.op("pool", lambda e: e.tensor_tensor(out=tA[:, 0:521], in0=tB[:, 0:521],
                                                                   in1=tB[:, 4:525], op=ALU.add), [tmpb], [tmpb])
                            P.op("pool", lambda e, ws=ws: e.tensor_tensor(out=ws, in0=tA[:, 0:512], in1=tA[:, 8:520],
                                                                          op=ALU.add), [tmpb], [wsb[g]])
                P.op("dve", lambda e, g=g, w=w, ug=ug, ws=ws: e.scalar_tensor_tensor(
                    out=mixed[:, g, :], in0=ug[:, 8:520], scalar=-float(w), in1=ws, op0=ALU.mult, op1=ALU.add),
                    [wsb[g], usb[i]], [mixb[g]])
                for side, cond, c0 in ((0, r0 == 0, 0), (1, r0 == 56, 504)):
                    if cond:
                        P.op("dve", lambda e, g=g, side=side, c0=c0, ws=ws: e.tensor_tensor(
                            out=t8, in0=ws[:, c0:c0 + 8], in1=invc4[:, g, side, :], op=ALU.mult),
                            [wsb[g], const_b], [t8b])
                        P.op("dve", lambda e, g=g, c0=c0, ug=ug, w=w: e.scalar_tensor_tensor(
                            out=mixed[:, g, c0:c0 + 8], in0=ug[:, 8 + c0:16 + c0], scalar=-float(w), in1=t8,
                            op0=ALU.mult, op1=ALU.add), [t8b, usb[i]], [mixb[g]])
            items = []
            for pi_ in range(4):
                r = r0 + 2 * pi_
                if r < 4:
                    kr0s = [0, 2, 4, 6]
                    edge = 1
                elif r > 58:
                    kr0s = [56, 58, 60, 62]
                    edge = 1
                else:
                    kr0s = [r - 4, r - 2, r, r + 2, r + 4]
                    edge = 0
                for h in range(8):
                    items.append((pi_, r, kr0s, edge, h))

            def qk(idx):
                pi_, r, kr0s, edge, h = items[idx]
                n = len(kr0s)
                sslot = (item + idx) % 2
                hp = slice((h % 2) * 64, (h % 2) * 64 + 64)

                def f(e):
                    ins = None
                    for j, kr0 in enumerate(kr0s):
                        c = (n - 1 - j) * 128
                        ko_ = (kr0 - klo) * 64
                        ins = e.matmul(Sap[sslot][:, c:c + 128], k_[hp, h // 2, ko_:ko_ + 128],
                                       q_[hp, h // 2, pi_ * 128:(pi_ + 1) * 128], start=True, stop=True)
                    return ins
                P.op("pe", f, [ksb[i], qsb[i]], Sb[sslot])

            qk(0)
            qk(1)
            for idx in range(32):
                pi_, r, kr0s, edge, h = items[idx]
                n = len(kr0s)
                sslot = (item + idx) % 2
                ps3 = (item + idx) % 3
                P.op("act", lambda e, sslot=sslot, n=n, ps3=ps3: e.activation(
                    out=pexp[ps3][:, 0:n * 128], in_=Sap[sslot][:, 0:n * 128], func=AF.Exp),
                    Sb[sslot], [pexpb[ps3]])
                e_start = 7 - (kr0s[-1] - r)
                toff = h * 1024 + e_start * 64
                P.op("dve", lambda e, n=n, toff=toff, edge=edge, ps3=ps3: e.tensor_tensor(
                    out=pm[ps3][:, 0:n * 128], in0=pexp[ps3][:, 0:n * 128],
                    in1=tabs[edge][:, toff:toff + n * 128], op=ALU.mult),
                    [pexpb[ps3], tabb], [pmb[ps3]])
                if idx + 2 < 32:
                    qk(idx + 2)
                ob = h // 4
                oo = (h % 4) * 65

                def pv(e, kr0s=kr0s, n=n, ps3=ps3, ob=ob, oo=oo, h=h):
                    ins = None
                    for j, kr0 in enumerate(kr0s):
                        c = (n - 1 - j) * 128
                        ins = e.matmul(Oap[ob][:, oo:oo + 65], pm[ps3][:, c:c + 128],
                                       v_[:, (kr0 - klo) // 2, h * 65:(h + 1) * 65],
                                       start=(j == 0), stop=(j == n - 1))
                    return ins
                P.op("pe", pv, [pmb[ps3], vsb[i]], [Ob[ob]])
                if h == 7:
                    for ob2 in range(2):
                        o3 = Oap[ob2][:, 0:260].rearrange("p (h f) -> p h f", h=4)
                        P.op("dve", lambda e, o3=o3, ob2=ob2: e.reciprocal(
                            out=rc[:, ob2 * 4:ob2 * 4 + 4], in_=o3[:, :, 64]), [Ob[ob2]], [rcb])
                        P.op("dve", lambda e, o3=o3, ob2=ob2: e.tensor_tensor(
                            out=ytok[:, ob2 * 256:(ob2 + 1) * 256].rearrange("p (h f) -> p h f", h=4),
                            in0=o3[:, :, 0:64],
                            in1=rc[:, ob2 * 4:ob2 * 4 + 4].unsqueeze(2).to_broadcast([128, 4, 64]), op=ALU.mult),
                            [Ob[ob2], rcb], [ytokb])

                    def tr(e):
                        ins = None
                        for c4 in range(4):
                            ins = e.transpose(psT[:, c4 * 128:(c4 + 1) * 128], ytok[:, c4 * 128:(c4 + 1) * 128], ident_bf)
                        return ins
                    P.op("pe", tr, [ytokb, const_b], [pstb])
                    P.op("act", lambda e, pi_=pi_: e.activation(
                        out=yT[:, 0:4, pi_ * 128:(pi_ + 1) * 128],
                        in_=psT[:, 0:512].rearrange("p (c t) -> p c t", c=4), func=AF.Identity), [pstb], [yTa])
                D.pop(2)
            item += 32
            D.flush()
            for g in range(4):
                pb = 4 + gp % 3
                gp += 1
                P.op("pe", lambda e, g=g, pb=pb: e.matmul(bank(pb), wpool[:, g, :], mixed[:, g, :], start=True, stop=True),
                     [woutb, mixb[g]], [psb[pb]])
                P.op("act", lambda e, g=g, pb=pb: e.activation(out=yT[:, 4 + g, :], in_=bank(pb), func=AF.Identity,
                                                               scale=vec(48 + l * 4 + g)), [psb[pb], const_b], [yTp[g]])
            for m in range(8):
                pb = 4 + gp % 3
                gp += 1

                def mo(e, m=m, pb=pb):
                    ins = None
                    for k in range(8):
                        ins = e.matmul(bank(pb), wout[:, k, m * 128:(m + 1) * 128], yT[:, k, :],
                                       start=(k == 0), stop=(k == 7))
                    return ins
                P.op("pe", mo, [woutb, yTa] + yTp, [psb[pb]])
                P.op("dve", lambda e, m=m, pb=pb: e.scalar_tensor_tensor(
                    out=xs1[:, m, :], in0=bank(pb), scalar=mv(l, 16 + m, sq), in1=xs1[:, m, :],
                    op0=ALU.mult, op1=ALU.add), [psb[pb], xsb1[m], mod_b], [xsb1[m]])

            def store(b=b):
                P.dma("sp", [(X_v[:, :, b * 512:(b + 1) * 512], os1)], osb1, [Xb[b]], osb1[0])
            fin = [store]
            if b + 1 < NBL:
                fin.append(lambda b=b: loadx(b + 1))
            pre, post = ln_steps(xs1, xsb1, 512, zbf, zsq, st, lambda m: vec(16 + l * 8 + m),
                                 lambda m: vec(32 + l * 8 + m), 1, os1, osb1, 5, 6, fin)
            for s_ in pre + post:
                D.add(s_)
        D.flush()

    def phaseC(l, final):
        A.off = PERSIST
        NT = 256
        NBC = T // NT
        w1 = A.bf16(8 * 4096).rearrange("p (k n) -> p k n", k=8)
        w2 = A.bf16(32 * 1024).rearrange("p (k n) -> p k n", k=32)
        w1b = P.buf("w1")
        w2b = P.buf("w2")
        P.dma("pool", [(w1[:, i:i + 1, :], w1_d[l, :, i:i + 1, :]) for i in range(8)], [], [w1b], w1b)
        P.dma("pool", [(w2[:, 4 * i:4 * i + 4, :], w2_d[l, :, 4 * i:4 * i + 4, :]) for i in range(8)], [], [w2b], w2b)
        xs = [A.f32(8 * NT).rearrange("p (m t) -> p m t", m=8) for _ in range(3)]
        xsb = [bufs("xc", 8), bufs("xc", 8), bufs("xc", 8)]
        h2 = A.bf16(8 * NT).rearrange("p (m t) -> p m t", m=8)
        h2b = bufs("h2", 8)
        hid = A.bf16(32 * NT).rearrange("p (m t) -> p m t", m=32)
        hidb = bufs("hid", 32)
        rl = [A.f32(NT), A.f32(NT)]
        rlb = [P.buf("rl"), P.buf("rl")]
        os_ = [A.f32(8 * NT).rearrange("p (m t) -> p m t", m=8) for _ in range(2)]
        osb = [bufs("oc", 8), bufs("oc", 8)]
        zbf = (A.bf16(8 * NT).rearrange("p (m t) -> p m t", m=8), bufs("zbf", 8))
        zsq = (A.bf16(8 * NT).rearrange("p (m t) -> p m t", m=8), bufs("zsq", 8))
        st = [A.f32(NT) for _ in range(4)]
        dst_v = outT_v if final else X_v
        NBL = NBC if lim is None else 2 * (lim - 1)
        D = Deferred()

        def load(b):
            P.dma("sp", [(xs[b % 3], X_v[:, :, b * NT:(b + 1) * NT])], [Xb[(b * NT) // 512]], xsb[b % 3],
                  xsb[b % 3][0])

        def hcomp(b):
            sq = (b * NT) // SEQ
            x_ = xs[b % 3]
            for m in range(8):
                if m % 2 == 0:
                    P.op("act", lambda e, m=m: e.activation(out=h2[:, m, :], in_=x_[:, m, :], func=AF.Identity,
                                                            bias=mv(l, 24 + m, sq), scale=mv(l, 32 + m, sq)),
                         [xsb[b % 3][m], mod_b], [h2b[m]])
                else:
                    P.op("dve", lambda e, m=m: e.tensor_scalar(out=h2[:, m, :], in0=x_[:, m, :],
                                                               scalar1=mv(l, 32 + m, sq), scalar2=mv(l, 24 + m, sq),
                                                               op0=ALU.mult, op1=ALU.add),
                         [xsb[b % 3][m], mod_b], [h2b[m]])
        load(0)
        hcomp(0)
        pi = 0
        for b in range(NBL):
            i = b % 2
            sq = (b * NT) // SEQ
            if b + 1 < NBL:
                load(b + 1)
            x_ = xs[b % 3]
            xb_ = xsb[b % 3]
            for jc in range(32):
                pb = pi % 5
                pi += 1

                def m1(e, jc=jc, pb=pb):
                    ins = None
                    for k in range(8):
                        ins = e.matmul(bank(pb)[:, 0:NT], w1[:, k, jc * 128:(jc + 1) * 128], h2[:, k, :],
                                       start=(k == 0), stop=(k == 7))
                    return ins
                P.op("pe", m1, [w1b] + h2b, [psb[pb]])
                ri = jc % 2
                P.op("act", lambda e, pb=pb, ri=ri: e.activation(out=rl[ri], in_=bank(pb)[:, 0:NT], func=AF.Relu),
                     [psb[pb]], [rlb[ri]])
                P.op("pool", lambda e, jc=jc, ri=ri: e.tensor_tensor(out=hid[:, jc, :], in0=rl[ri], in1=rl[ri],
                                                                     op=ALU.mult), [rlb[ri]], [hidb[jc]])
                D.pop(2)
            D.flush()
            if b + 1 < NBL:
                hcomp(b + 1)
            for m in range(8):
                pb = pi % 5
                pi += 1

                def m2(e, m=m, pb=pb):
                    ins = None
                    for k in range(32):
                        ins = e.matmul(bank(pb)[:, 0:NT], w2[:, k, m * 128:(m + 1) * 128], hid[:, k, :],
                                       start=(k == 0), stop=(k == 31))
                    return ins
                P.op("pe", m2, [w2b] + hidb, [psb[pb]])
                P.op("dve", lambda e, m=m, pb=pb: e.scalar_tensor_tensor(
                    out=x_[:, m, :], in0=bank(pb)[:, 0:NT], scalar=mv(l, 40 + m, sq), in1=x_[:, m, :],
                    op0=ALU.mult, op1=ALU.add), [psb[pb], xb_[m], mod_b], [xb_[m]])
            wr = [] if final else [Xb[(b * NT) // 512]]

            def store(b=b, i=i, wr=wr):
                P.dma("sp", [(dst_v[:, :, b * NT:(b + 1) * NT], os_[i])], osb[i], wr, osb[i][0])
            pre, post = ln_steps(x_, xb_, NT, zbf, zsq, st, lambda m: vec(56 + l * 8 + m),
                                 lambda m: vec(72 + l * 8 + m), 1, os_[i], osb[i], 5, 6, [store])
            for s_ in pre + post:
                D.add(s_)
        D.flush()

    phases = [("0", phase0)]
    for l in range(L):
        phases += [("A%d" % l, lambda l=l: phaseA(l)), ("B%d" % l, lambda l=l: phaseB(l)),
                   ("C%d" % l, lambda l=l: phaseC(l, l == L - 1))]
    for name, fn in phases:
        P.barrier()
        fn()
        if stop_after == name:
            break
    P.barrier()
    return nc


def _chunk_vec(v):
    return np.ascontiguousarray(v.reshape(-1, 128).T)


def _bias_tables(rpb_l):
    out = np.full((2, 128, 8, 16, 64), NEG, np.float32)
    c = np.arange(64)
    cs = np.clip(c - 8, 0, 48)
    for kl in range(2):
        for e in range(16):
            rel = 14 - e + kl
            if rel < 0 or rel > 14:
                continue
            for kc in range(64):
                valid = (kc >= cs) & (kc < cs + 16)
                idx = kc - c + 15
                cc = c[valid]
                vals = rpb_l[:, rel, idx[valid]]
                out[1, kl * 64 + kc, :, e, cc] = vals.T
                if 3 <= rel <= 10:
                    out[0, kl * 64 + kc, :, e, cc] = vals.T
    return out.reshape(2, 128, 8192)


def _prep_shared(inp):
    sh = {}
    sh["w_ada"] = np.ascontiguousarray(inp["w_ada"].reshape(L, 8, 128, 6 * D).transpose(0, 2, 1, 3))
    sh["b_ada"] = np.ascontiguousarray(inp["b_ada"].reshape(L, 48, 128).transpose(2, 0, 1))
    sh["w_in"] = np.ascontiguousarray(inp["w_in"].reshape(L, 8, 128, 2048).transpose(0, 2, 1, 3))
    sh["w_out"] = np.ascontiguousarray(inp["w_out"].reshape(L, 8, 128, 1024).transpose(0, 2, 1, 3))
    sh["w_mlp1"] = np.ascontiguousarray(inp["w_mlp1"].reshape(L, 8, 128, 4096).transpose(0, 2, 1, 3))
    sh["w_mlp2"] = np.ascontiguousarray(inp["w_mlp2"].reshape(L, 32, 128, 1024).transpose(0, 2, 1, 3))
    sh["w_pool"] = np.ascontiguousarray(inp["w_pool"].transpose(0, 2, 1, 3))
    vecs = np.zeros((128, 88), np.float32)
    vecs[:, 0:8] = _chunk_vec(inp["ln_in_g"])
    vecs[:, 8:16] = _chunk_vec(inp["ln_in_b"])
    for l in range(L):
        vecs[:, 16 + l * 8:24 + l * 8] = _chunk_vec(inp["ln1_g"][l])
        vecs[:, 32 + l * 8:40 + l * 8] = _chunk_vec(inp["ln1_b"][l])
        vecs[:, 48 + l * 4:52 + l * 4] = inp["pool_scale"][l].reshape(4, 128).T
        vecs[:, 56 + l * 8:64 + l * 8] = _chunk_vec(inp["ln2_g"][l])
        vecs[:, 72 + l * 8:80 + l * 8] = _chunk_vec(inp["ln2_b"][l])
    sh["vecs"] = vecs
    sh["biasT"] = np.stack([_bias_tables(np.asarray(inp["rpb"][l])) for l in range(L)])
    sh["ident"] = np.eye(128, dtype=np.float32)
    invc = np.zeros((4, 2, 8), np.float32)
    for g, w in enumerate((2, 4, 8, 16)):
        for side, toks in ((0, np.arange(0, 8)), (1, np.arange(SEQ - 8, SEQ))):
            lo = np.clip(toks - w // 2, 0, SEQ)
            hi = np.clip(toks - w // 2 + w, 0, SEQ)
            invc[g, side] = w / (hi - lo)
    sh["invc"] = np.ascontiguousarray(np.broadcast_to(invc.reshape(1, 64), (128, 64)))
    return sh


VEC_COLS = 88


def _core_inputs(inp, sh, core):
    b0 = core * NSEQ
    x2 = np.asarray(inp["x"][b0:b0 + NSEQ]).reshape(T, D)
    m = dict(sh)
    m["xT"] = np.ascontiguousarray(x2.T)
    c2 = np.asarray(inp["c"][b0:b0 + NSEQ])
    m["cT"] = np.ascontiguousarray(c2.reshape(NSEQ, 8, 128).transpose(2, 1, 0))
    return m


_NC_CACHE = {}


def kernel(**inputs):
    inp = {k: np.asarray(v) for k, v in inputs.items()}
    sh = _prep_shared(inp)
    if "nc" not in _NC_CACHE:
        _NC_CACHE["nc"] = build_nc()
    nc = _NC_CACHE["nc"]
    in_maps = [_core_inputs(inp, sh, c) for c in range(NCORES)]
    res = run_bass_kernel_spmd(nc, in_maps, core_ids=list(range(NCORES)))
    out = np.empty((NCORES * NSEQ, SEQ, D), np.float32)
    for c in range(NCORES):
        oT = np.asarray(res.results[c]["outT"])
        out[c * NSEQ:(c + 1) * NSEQ] = oT.T.reshape(NSEQ, SEQ, D)
    return out
```

```python
import numpy as np
import concourse.bass as bass
import concourse.mybir as mybir
from concourse.bass_utils import run_bass_kernel_spmd

F32 = mybir.dt.float32
BF16 = mybir.dt.bfloat16
AF = mybir.ActivationFunctionType
ALU = mybir.AluOpType

NCORES = 8
D = 1024
SEQ = 4096
NSEQ = 2
T = NSEQ * SEQ
L = 2
ALPHA = (2.0 * L) ** 0.25
EPS_P = 1e-5 / (ALPHA * ALPHA)
NEG = -80.0
SBUF_WORDS = 52500


class Buf:
    __slots__ = ("name", "w", "r", "dsem", "dcnt")

    def __init__(self, name):
        self.name = name
        self.w = None
        self.r = {}
        self.dsem = None
        self.dcnt = 0


class Prog:
    def __init__(self, nc):
        self.nc = nc
        self.eng = {"pe": nc.tensor, "act": nc.scalar, "dve": nc.vector, "pool": nc.gpsimd, "sp": nc.sync}
        self.semobj = {}
        self.cnt = {}
        for k in ("pe", "act", "dve", "pool"):
            self.semobj[k] = nc.alloc_semaphore("s_" + k)
            self.cnt[k] = 0
        self.seen = {k: {} for k in self.eng}
        self.dbufs = []
        self.nbuf = 0

    def buf(self, name):
        self.nbuf += 1
        return Buf("%s_%d" % (name, self.nbuf))

    def _deps(self, e, reads, writes):
        need = {}

        def add(k, v):
            if e == "pe" and k == "pe":
                return
            if need.get(k, 0) < v:
                need[k] = v

        for b in reads:
            if b.w is not None:
                add(*b.w)
        for b in writes:
            if b.w is not None:
                add(*b.w)
            for k, v in b.r.items():
                add(k, v)
        seen = self.seen[e]
        for k, v in need.items():
            if seen.get(k, 0) >= v:
                continue
            self.eng[e].wait_ge(self.semobj[k], v)
            seen[k] = v

    def _mark(self, ev, reads, writes):
        k, v = ev
        for b in reads:
            if b.r.get(k, 0) < v:
                b.r[k] = v
        for b in writes:
            b.w = ev
            b.r = {}

    def op(self, e, fn, reads=(), writes=()):
        self._deps(e, reads, writes)
        ins = fn(self.eng[e])
        self.cnt[e] += 1
        ins.then_inc(self.semobj[e], 1)
        self._mark((e, self.cnt[e]), reads, writes)

    def dma(self, q, pairs, reads, writes, sembuf):
        self._deps(q, reads, writes)
        if sembuf.dsem is None:
            sembuf.dsem = self.nc.alloc_semaphore("d_" + sembuf.name)
            self.semobj["d:" + sembuf.name] = sembuf.dsem
            self.dbufs.append(sembuf)
        for o, i in pairs:
            ins = self.eng[q].dma_start(out=o, in_=i)
            sembuf.dcnt += 16
            ins.then_inc(sembuf.dsem, 16)
        self._mark(("d:" + sembuf.name, sembuf.dcnt), reads, writes)

    def barrier(self, engines=("pe", "act", "dve", "pool", "sp")):
        evs = [(k, self.cnt[k]) for k in ("pe", "act", "dve", "pool") if self.cnt[k] > 0]
        evs += [("d:" + b.name, b.dcnt) for b in self.dbufs if b.dcnt > 0]
        for e in engines:
            seen = self.seen[e]
            for k, v in evs:
                if seen.get(k, 0) >= v:
                    continue
                self.eng[e].wait_ge(self.semobj[k], v)
                seen[k] = v


class Arena:
    def __init__(self, big):
        self.big = big
        self.off = 0

    def f32(self, n):
        a = self.big[:, self.off:self.off + n]
        self.off += n
        assert self.off <= SBUF_WORDS, self.off
        return a

    def bf16(self, n):
        w = (n + 1) // 2
        a = self.big[:, self.off:self.off + w].bitcast(BF16)
        self.off += w
        assert self.off <= SBUF_WORDS, self.off
        return a


def build_nc(debug=False, stop_after=None, lim=None):
    nc = bass.Bass("TRN2", target_bir_lowering=False)
    P = Prog(nc)

    def din(name, shape, dt=F32):
        return nc.dram_tensor(name, list(shape), dt, kind="ExternalInput").ap()

    xT_d = din("xT", [D, T])
    cT_d = din("cT", [128, 8, NSEQ])
    wada_d = din("w_ada", [L, 128, 8, 6 * D])
    bada_d = din("b_ada", [128, L, 48])
    win_d = din("w_in", [L, 128, 8, 2048])
    wout_d = din("w_out", [L, 128, 8, 1024])
    w1_d = din("w_mlp1", [L, 128, 8, 4096])
    w2_d = din("w_mlp2", [L, 128, 32, 1024])
    wpool_d = din("w_pool", [L, 128, 4, 128])
    vecs_d = din("vecs", [128, 88])
    biasT_d = din("biasT", [L, 2, 128, 8192])
    ident_d = din("ident", [128, 128])
    invc_d = din("invc", [128, 64])
    outT_d = nc.dram_tensor("outT", [D, T], F32, kind="ExternalOutput").ap()
    skind = "ExternalOutput" if debug else "Internal"
    X_d = nc.dram_tensor("Xs", [D, T], F32, kind=skind).ap()
    Q_d = nc.dram_tensor("Qs", [512, T], BF16, kind=skind).ap()
    K_d = nc.dram_tensor("Ks", [512, T], BF16, kind=skind).ap()
    U_d = nc.dram_tensor("Us", [512, T], F32, kind=skind).ap()
    V_d = nc.dram_tensor("Vs", [T, 520], BF16, kind=skind).ap()

    xT_v = xT_d.rearrange("(m p) t -> p m t", p=128)
    outT_v = outT_d.rearrange("(m p) t -> p m t", p=128)
    X_v = X_d.rearrange("(m p) t -> p m t", p=128)
    Q_v = Q_d.rearrange("(m p) t -> p m t", p=128)
    K_v = K_d.rearrange("(m p) t -> p m t", p=128)
    U_v = U_d.rearrange("(m p) t -> p m t", p=128)
    V_v = V_d.rearrange("(n p) f -> p n f", p=128)

    big = nc.alloc_sbuf_tensor("big", [128, SBUF_WORDS], F32)
    psA = nc.alloc_psum_tensor("psA", [128, 3584], F32)
    psT = nc.alloc_psum_tensor("psT", [128, 1024], BF16)
    A = Arena(big)

    NB = T // 512
    Xb = [P.buf("Xd") for _ in range(NB)]
    Qb = [P.buf("Qd") for _ in range(NB)]
    Kb = [P.buf("Kd") for _ in range(NB)]
    Ub = [P.buf("Ud") for _ in range(NB)]
    Vb = [P.buf("Vd") for _ in range(NB)]

    psb = [P.buf("ps") for _ in range(7)]
    pstb = P.buf("pst")

    def bank(i, n=512):
        return psA[:, i * 512:i * 512 + n]

    ones_bf = A.bf16(128)
    ident_bf = A.bf16(128)
    vecs = A.f32(88)
    invc = A.f32(64)
    modv = A.f32(L * 48 * NSEQ)
    cact = A.f32(8 * NSEQ)
    bada = A.f32(L * 48)
    const_b = P.buf("const")
    mod_b = P.buf("mod")
    epsv = A.f32(2)
    PERSIST = A.off

    modv4 = modv.rearrange("p (l j s) -> p l j s", l=L, j=48)

    def mv(l, j, s):
        return modv4[:, l, j, s:s + 1]

    def vec(i):
        return vecs[:, i:i + 1]
    P.dma("sp", [(vecs, vecs_d[:, :]), (invc, invc_d[:, :]), (cact, cT_d.rearrange("p k s -> p (k s)")),
                 (bada, bada_d.rearrange("p l j -> p (l j)"))], [], [const_b], const_b)
    ident_st = A.f32(128)
    P.dma("sp", [(ident_st, ident_d[:, :])], [], [const_b], const_b)
    P.op("dve", lambda e: e.tensor_copy(out=ident_bf, in_=ident_st), [const_b], [const_b])
    P.op("dve", lambda e: e.memset(ones_bf, 1.0 / D), [], [const_b])
    P.op("act", lambda e: e.activation(out=cact, in_=cact, func=AF.Silu), [const_b], [const_b])

    class Deferred:
        def __init__(self):
            self.q = []

        def add(self, fn):
            self.q.append(fn)

        def pop(self, n=1):
            for _ in range(n):
                if self.q:
                    self.q.pop(0)()

        def flush(self):
            while self.q:
                self.q.pop(0)()

    def bufs(name, n):
        return [P.buf(name) for _ in range(n)]

    P.op("dve", lambda e: e.memset(epsv[:, 0:1], 1e-5), [], [const_b])
    P.op("dve", lambda e: e.memset(epsv[:, 1:2], EPS_P), [], [const_b])
    for l in range(L):
        for g, w in enumerate((2, 4, 8, 16)):
            c = 48 + l * 4 + g
            P.op("dve", lambda e, c=c, w=w: e.tensor_scalar(out=vecs[:, c:c + 1], in0=vecs[:, c:c + 1],
                                                            scalar1=1.0 / w, scalar2=None, op0=ALU.mult),
                 [const_b], [const_b])

    def eps_ap(eps):
        return epsv[:, 0:1] if eps == 0 else epsv[:, 1:2]

    st_b = [P.buf("st") for _ in range(4)]

    def ln_steps(z, zb_, N, zbf, zsq, st, gam, bet, eps, out, outb, ps1, ps2, final_steps=()):
        msq, sd, Aa, Bb = st
        stb = st_b
        pre = []
        for m in range(8):
            pre.append(lambda m=m: P.op("act", lambda e: e.activation(
                out=zsq[0][:, m, :], in_=z[:, m, :], func=AF.Square), [zb_[m]], [zsq[1][m]]))
            pre.append(lambda m=m: P.op("dve", lambda e: e.tensor_copy(out=zbf[0][:, m, :], in_=z[:, m, :]),
                                        [zb_[m]], [zbf[1][m]]))

        def l2a():
            def s1(e):
                ins = None
                for m in range(8):
                    ins = e.matmul(bank(ps1)[:, 0:N], ones_bf, zbf[0][:, m, :], start=(m == 0), stop=(m == 7))
                return ins

            def s2(e):
                ins = None
                for m in range(8):
                    ins = e.matmul(bank(ps2)[:, 0:N], ones_bf, zsq[0][:, m, :], start=(m == 0), stop=(m == 7))
                return ins
            P.op("pe", s1, zbf[1] + [const_b], [psb[ps1]])
            P.op("pe", s2, zsq[1] + [const_b], [psb[ps2]])
        pre.append(l2a)
        pre.append(lambda: P.op("act", lambda e: e.activation(out=msq, in_=bank(ps1)[:, 0:N], func=AF.Square),
                                [psb[ps1]], [stb[0]]))
        pre.append(lambda: P.op("dve", lambda e: e.tensor_tensor(out=sd, in0=bank(ps2)[:, 0:N], in1=msq,
                                                                 op=ALU.subtract), [psb[ps2], stb[0]], [stb[1]]))
        pre.append(lambda: P.op("act", lambda e: e.activation(out=sd, in_=sd, func=AF.Ln, bias=eps_ap(eps), scale=1.0),
                                [stb[1], const_b], [stb[1]]))
        pre.append(lambda: P.op("act", lambda e: e.activation(out=Aa, in_=sd, func=AF.Exp, scale=-0.5),
                                [stb[1]], [stb[2]]))
        pre.append(lambda: P.op("dve", lambda e: e.scalar_tensor_tensor(
            out=Bb, in0=bank(ps1)[:, 0:N], scalar=-1.0, in1=Aa, op0=ALU.mult, op1=ALU.mult),
            [psb[ps1], stb[2]], [stb[3]]))
        post = []

        def zA(m):
            P.op("dve", lambda e: e.tensor_tensor(out=z[:, m, :], in0=z[:, m, :], in1=Aa, op=ALU.mult),
                 [zb_[m], stb[2]], [zb_[m]])

        def zB(m):
            P.op("pool", lambda e: e.tensor_tensor(out=z[:, m, :], in0=z[:, m, :], in1=Bb, op=ALU.add),
                 [zb_[m], stb[3]], [zb_[m]])

        def aff(m):
            P.op("act", lambda e: e.activation(out=out[:, m, :], in_=z[:, m, :], func=AF.Identity,
                                               bias=bet(m), scale=gam(m)), [zb_[m], const_b], [outb[m]])
        for t in range(12):
            if t < 8:
                post.append(lambda m=t: zA(m))
            if 2 <= t < 10:
                post.append(lambda m=t - 2: zB(m))
            if 4 <= t < 12:
                post.append(lambda m=t - 4: aff(m))
        post.extend(final_steps)
        return pre, post

    def phase0():
        A.off = PERSIST
        wsl = [A.f32(8 * 512), A.f32(8 * 512)]
        wslb = [P.buf("wsl"), P.buf("wsl")]
        cact3 = cact.rearrange("p (k s) -> p k s", k=8)
        MD = Deferred()

        def mstep(l, sl, it):
            wb3 = wsl[it % 2].rearrange("p (k n) -> p k n", k=8)
            wbb = wslb[it % 2]
            P.dma("sp", [(wb3, wada_d[l, :, :, sl * 512:(sl + 1) * 512])], [], [wbb], wbb)
            for jj in range(4):
                j = sl * 4 + jj
                col = (l * 48 + j) * NSEQ

                def mm(e, jj=jj, col=col):
                    ins = None
                    for k in range(8):
                        ins = e.matmul(bank(0)[:, col:col + NSEQ], wb3[:, k, jj * 128:(jj + 1) * 128],
                                       cact3[:, k, :], start=(k == 0), stop=(k == 7))
                    return ins
                P.op("pe", mm, [wbb, const_b], [psb[0]])
        it = 0
        for l in range(L):
            for sl in range(12):
                MD.add(lambda l=l, sl=sl, it=it: mstep(l, sl, it))
                it += 1

        xs = [A.f32(8 * 512).rearrange("p (m t) -> p m t", m=8) for _ in range(3)]
        xsb = [bufs("x0", 8), bufs("x0", 8), bufs("x0", 8)]
        os_ = [A.f32(8 * 512).rearrange("p (m t) -> p m t", m=8) for _ in range(2)]
        osb = [bufs("o0", 8), bufs("o0", 8)]
        zbf = (A.bf16(8 * 512).rearrange("p (m t) -> p m t", m=8), bufs("zbf", 8))
        zsq = (A.bf16(8 * 512).rearrange("p (m t) -> p m t", m=8), bufs("zsq", 8))
        st = [A.f32(512) for _ in range(4)]
        NBL = NB if lim is None else lim

        def load(b):
            P.dma("sp", [(xs[b % 3], xT_v[:, :, b * 512:(b + 1) * 512])], [], xsb[b % 3], xsb[b % 3][0])
        load(0)
        prev = []
        for b in range(NBL):
            if b + 1 < NBL:
                load(b + 1)
            MD.pop(2)
            i = b % 2

            def store(b=b, i=i):
                P.dma("sp", [(X_v[:, :, b * 512:(b + 1) * 512], os_[i])], osb[i], [Xb[b]], osb[i][0])
            pre, post = ln_steps(xs[b % 3], xsb[b % 3], 512, zbf, zsq, st, lambda m: vec(m), lambda m: vec(8 + m), 0,
                                 os_[i], osb[i], 5, 6, [store])
            k = 0
            for k in range(17):
                pre[k]()
                if k < len(prev):
                    prev[k]()
            for p_ in prev[17:]:
                p_()
            for p_ in pre[17:]:
                p_()
            prev = post
        for p_ in prev:
            p_()
        MD.flush()
        P.op("dve", lambda e: e.tensor_tensor(
            out=modv.rearrange("p (a s) -> p a s", s=NSEQ),
            in0=bank(0)[:, 0:L * 48 * NSEQ].rearrange("p (a s) -> p a s", s=NSEQ),
            in1=bada.unsqueeze(2).to_broadcast([128, L * 48, NSEQ]), op=ALU.add), [psb[0], const_b], [mod_b])
        for l in range(L):
            for j0, mul in ((8, 1.0), (32, 1.0), (16, 1.0 / ALPHA), (40, 1.0 / ALPHA)):
                sl_ = modv4[:, l, j0:j0 + 8, :]
                P.op("dve", lambda e, sl_=sl_, mul=mul: e.tensor_scalar(
                    out=sl_, in0=sl_, scalar1=1.0, scalar2=mul, op0=ALU.add, op1=ALU.mult), [mod_b], [mod_b])

    def phaseA(l):
        A.off = PERSIST
        win = A.bf16(8 * 2048).rearrange("p (k n) -> p k n", k=8)
        winb = bufs("win", 4)
        for cg in (0, 1, 3, 2):
            P.dma("pool", [(win[:, :, cg * 512:(cg + 1) * 512], win_d[l, :, :, cg * 512:(cg + 1) * 512])],
                  [], [winb[cg]], winb[cg])
        xs = [A.f32(8 * 512).rearrange("p (m t) -> p m t", m=8) for _ in range(2)]
        xsb = [bufs("xa", 8), bufs("xa", 8)]
        hTs = [A.bf16(8 * 512).rearrange("p (m t) -> p m t", m=8) for _ in range(2)]
        hTbs = [bufs("hT", 8), bufs("hT", 8)]
        qo = [A.bf16(4 * 512).rearrange("p (m t) -> p m t", m=4) for _ in range(2)]
        ko = [A.bf16(4 * 512).rearrange("p (m t) -> p m t", m=4) for _ in range(2)]
        uo = [A.f32(4 * 512).rearrange("p (m t) -> p m t", m=4) for _ in range(2)]
        vo = [A.bf16(4 * 520).rearrange("p (n f) -> p n f", n=4) for _ in range(2)]
        qob = [bufs("qo", 4), bufs("qo", 4)]
        kob = [bufs("ko", 4), bufs("ko", 4)]
        uob = [bufs("uo", 4), bufs("uo", 4)]
        vob = [bufs("vo", 4), bufs("vo", 4)]
        for i in range(2):
            P.op("dve", lambda e, i=i: e.memset(vo[i], 1.0), [], vob[i])
        NBL = NB if lim is None else lim

        def load(b):
            P.dma("sp", [(xs[b % 2], X_v[:, :, b * 512:(b + 1) * 512])], [Xb[b]], xsb[b % 2], xsb[b % 2][0])

        def hcomp(b):
            s = b // 8
            x = xs[b % 2]
            hT = hTs[b % 2]
            for m in range(8):
                if m % 2 == 0:
                    P.op("act", lambda e, m=m: e.activation(out=hT[:, m, :], in_=x[:, m, :], func=AF.Identity,
                                                            bias=mv(l, m, s), scale=mv(l, 8 + m, s)),
                         [xsb[b % 2][m], mod_b], [hTbs[b % 2][m]])
                else:
                    P.op("dve", lambda e, m=m: e.tensor_scalar(out=hT[:, m, :], in0=x[:, m, :],
                                                               scalar1=mv(l, 8 + m, s), scalar2=mv(l, m, s),
                                                               op0=ALU.mult, op1=ALU.add),
                         [xsb[b % 2][m], mod_b], [hTbs[b % 2][m]])
        load(0)
        if NBL > 1:
            load(1)
        hcomp(0)
        pi = 0
        for b in range(NBL):
            hT = hTs[b % 2]
            hTb = hTbs[b % 2]
            sl = b % 2
            gi = 0
            for grp, (j0, dst, dstb) in enumerate(((0, qo[sl], qob[sl]), (4, ko[sl], kob[sl]), (12, uo[sl], uob[sl]))):
                for jj in range(4):
                    j = j0 + jj
                    pb = pi % 5
                    pi += 1

                    def mm(e, j=j, pb=pb):
                        ins = None
                        for k in range(8):
                            ins = e.matmul(bank(pb), win[:, k, j * 128:(j + 1) * 128], hT[:, k, :],
                                           start=(k == 0), stop=(k == 7))
                        return ins
                    P.op("pe", mm, [winb[j // 4]] + hTb, [psb[pb]])
                    if grp == 0:
                        P.op("act", lambda e, jj=jj, pb=pb, dst=dst: e.activation(
                            out=dst[:, jj, :], in_=bank(pb), func=AF.Identity, scale=0.125), [psb[pb]], [dstb[jj]])
                    elif grp == 1:
                        P.op("dve", lambda e, jj=jj, pb=pb, dst=dst: e.tensor_copy(out=dst[:, jj, :], in_=bank(pb)),
                             [psb[pb]], [dstb[jj]])
                    else:
                        P.op("act", lambda e, jj=jj, pb=pb, dst=dst: e.activation(
                            out=dst[:, jj, :], in_=bank(pb), func=AF.Identity), [psb[pb]], [dstb[jj]])
                    gi += 1
                    if gi == 4 and b + 1 < NBL:
                        hcomp(b + 1)
            for tt in range(4):
                pb = pi % 5
                pi += 1

                def mmv(e, tt=tt, pb=pb):
                    ins = None
                    for k in range(8):
                        ins = e.matmul(bank(pb), hT[:, k, tt * 128:(tt + 1) * 128], win[:, k, 1024:1536],
                                       start=(k == 0), stop=(k == 7))
                    return ins
                P.op("pe", mmv, [winb[2]] + hTb, [psb[pb]])
                P.op("dve", lambda e, tt=tt, pb=pb: e.tensor_copy(
                    out=vo[sl][:, tt, :].rearrange("p (h f) -> p h f", h=8)[:, :, 0:64],
                    in_=bank(pb).rearrange("p (h f) -> p h f", h=8)), [psb[pb]], [vob[sl][tt]])
            ts = slice(b * 512, (b + 1) * 512)
            P.dma("sp", [(Q_v[:, :, ts], qo[sl])], qob[sl], [Qb[b]], qob[sl][0])
            P.dma("sp", [(K_v[:, :, ts], ko[sl])], kob[sl], [Kb[b]], kob[sl][0])
            P.dma("sp", [(U_v[:, :, ts], uo[sl])], uob[sl], [Ub[b]], uob[sl][0])
            P.dma("sp", [(V_v[:, b * 4:(b + 1) * 4, :], vo[sl])], vob[sl], [Vb[b]], vob[sl][0])
            if b + 2 < NBL:
                load(b + 2)

    def phaseB(l):
        A.off = PERSIST
        wout = A.bf16(8 * 1024).rearrange("p (k n) -> p k n", k=8)
        woutb = P.buf("wout")
        P.dma("pool", [(wout[:, 4 * i:4 * i + 4, :], wout_d[l, :, 4 * i:4 * i + 4, :]) for i in range(2)],
              [], [woutb], woutb)
        wpool = A.bf16(4 * 128).rearrange("p (g n) -> p g n", g=4)
        P.dma("pool", [(wpool, wpool_d[l])], [], [woutb], woutb)
        tabs = [A.bf16(8192), A.bf16(8192)]
        tabb = P.buf("tab")
        tst = [A.f32(2048), A.f32(2048)]
        tstb = [P.buf("tst"), P.buf("tst")]
        it = 0
        for ti in range(2):
            for q4 in range(4):
                P.dma("sp", [(tst[it % 2], biasT_d[l, ti, :, q4 * 2048:(q4 + 1) * 2048])], [], [tstb[it % 2]],
                      tstb[it % 2])
                P.op("act", lambda e, ti=ti, q4=q4, it=it: e.activation(
                    out=tabs[ti][:, q4 * 2048:(q4 + 1) * 2048], in_=tst[it % 2], func=AF.Exp),
                    [tstb[it % 2]], [tabb])
                it += 1
        P.barrier()
        A.off -= 4096
        qs = [A.bf16(4 * 512).rearrange("p (m t) -> p m t", m=4) for _ in range(2)]
        ks = [A.bf16(4 * 1024).rearrange("p (m t) -> p m t", m=4) for _ in range(2)]
        vs = [A.bf16(8 * 520).rearrange("p (n f) -> p n f", n=8) for _ in range(2)]
        us = [A.f32(4 * 528).rearrange("p (g t) -> p g t", g=4) for _ in range(2)]
        xs1 = A.f32(8 * 512).rearrange("p (m t) -> p m t", m=8)
        qsb = [P.buf("qs"), P.buf("qs")]
        ksb = [P.buf("ks"), P.buf("ks")]
        vsb = [P.buf("vs"), P.buf("vs")]
        usb = [P.buf("us"), P.buf("us")]
        xsb1 = bufs("xb", 8)
        yT = A.bf16(8 * 512).rearrange("p (m t) -> p m t", m=8)
        yTa = P.buf("yTa")
        yTp = bufs("yTp", 4)
        pexp = [A.bf16(640) for _ in range(3)]
        pexpb = bufs("pexp", 3)
        pm = [A.bf16(640) for _ in range(3)]
        pmb = bufs("pm", 3)
        rc = A.f32(8)
        rcb = P.buf("rc")
        ytok = A.bf16(512)
        ytokb = P.buf("ytok")
        tA = A.f32(528)
        tB = A.f32(528)
        wsv = [A.f32(512) for _ in range(4)]
        wsb = bufs("ws", 4)
        t8 = A.f32(8)
        t8b = P.buf("t8")
        tmpb = P.buf("ptmp")
        mixed = A.bf16(4 * 512).rearrange("p (g t) -> p g t", g=4)
        mixb = bufs("mixed", 4)
        os1 = A.f32(8 * 512).rearrange("p (m t) -> p m t", m=8)
        osb1 = bufs("ob", 8)
        zbf = (A.bf16(8 * 512).rearrange("p (m t) -> p m t", m=8), bufs("zbf", 8))
        zsq = (A.bf16(8 * 512).rearrange("p (m t) -> p m t", m=8), bufs("zsq", 8))
        st = [A.f32(512) for _ in range(4)]
        Sb = [[psb[0], psb[1]], [psb[2], psb[3]]]
        Sap = [psA[:, 0:640], psA[:, 1024:1664]]
        Ob = [psb[4], psb[5]]
        Oap = [bank(4), bank(5)]
        invc4 = invc.rearrange("p (g s t) -> p g s t", g=4, s=2)
        NBL = NB if lim is None else lim - 1
        D = Deferred()

        def krange(b):
            r0 = (b % 8) * 8
            return max(0, r0 - 4), min(64, r0 + 12)

        def load(b):
            i = b % 2
            sq = b // 8
            r0 = (b % 8) * 8
            klo, khi = krange(b)
            t0 = sq * SEQ + klo * 64
            nt = (khi - klo) * 64
            kblocks = sorted(set([(t0) // 512, (t0 + nt - 1) // 512, b]))
            P.dma("sp", [(qs[i], Q_v[:, :, b * 512:(b + 1) * 512])], [Qb[b]], [qsb[i]], qsb[i])
            P.dma("sp", [(ks[i][:, :, 0:nt], K_v[:, :, t0:t0 + nt])], [Kb[kb] for kb in kblocks], [ksb[i]], ksb[i])
            P.dma("sp", [(vs[i][:, 0:nt // 128, :], V_v[:, t0 // 128:(t0 + nt) // 128, :])],
                  [Vb[kb] for kb in kblocks], [vsb[i]], vsb[i])
            tb0 = b * 512
            lo = tb0 - 8
            hi = tb0 + 520
            dlo = 0
            if r0 == 0:
                P.op("pool", lambda e, i=i: e.memset(us[i][:, :, 0:8], 0.0), [], [usb[i]])
                lo = tb0
                dlo = 8
            if r0 == 56:
                P.op("pool", lambda e, i=i: e.memset(us[i][:, :, 520:528], 0.0), [], [usb[i]])
                hi = tb0 + 512
            ublocks = sorted(set([lo // 512, (hi - 1) // 512, b]))
            P.dma("sp", [(us[i][:, :, dlo:dlo + hi - lo], U_v[:, :, lo:hi])], [Ub[kb] for kb in ublocks],
                  [usb[i]], usb[i])

        def loadx(b):
            P.dma("sp", [(xs1, X_v[:, :, b * 512:(b + 1) * 512])], [Xb[b]], xsb1, xsb1[0])

        load(0)
        loadx(0)
        item = 0
        gp = 0
        for b in range(NBL):
            i = b % 2
            sq = b // 8
            r0 = (b % 8) * 8
            klo, khi = krange(b)
            if b + 1 < NBL:
                load(b + 1)
            q_, k_, v_, u_ = qs[i], ks[i], vs[i], us[i]
            for g, w in enumerate((2, 4, 8, 16)):
                ug = u_[:, g, :]
                ws = wsv[g]
                if g == 0:
                    P.op("pool", lambda e, ug=ug, ws=ws: e.tensor_tensor(out=ws, in0=ug[:, 7:519], in1=ug[:, 8:520],
                                                                         op=ALU.add), [usb[i]], [wsb[g]])
                else:
                    P.op("pool", lambda e, ug=ug: e.tensor_tensor(out=tA[:, 0:527], in0=ug[:, 0:527], in1=ug[:, 1:528],
                                                                  op=ALU.add), [usb[i]], [tmpb])
                    if g == 1:
                        P.op("pool", lambda e, ws=ws: e.tensor_tensor(out=ws, in0=tA[:, 6:518], in1=tA[:, 8:520],
                                                                      op=ALU.add), [tmpb], [wsb[g]])
                    else:
                        P.op("pool", lambda e: e.tensor_tensor(out=tB[:, 0:525], in0=tA[:, 0:525], in1=tA[:, 2:527],
                                                               op=ALU.add), [tmpb], [tmpb])
                        if g == 2:
                            P.op("pool", lambda e, ws=ws: e.tensor_tensor(out=ws, in0=tB[:, 4:516], in1=tB[:, 8:520],
                                                                          op=ALU.add), [tmpb], [wsb[g]])
                        else:
                            P.op("pool", lambda e: e.tensor_tensor(out=tA[:, 0:521], in0=tB[:, 0:521],
                                                                   in1=tB[:, 4:525], op=ALU.add), [tmpb], [tmpb])
                            P.op("pool", lambda e, ws=ws: e.tensor_tensor(out=ws, in0=tA[:, 0:512], in1=tA[:, 8:520],
                                                                          op=ALU.add), [tmpb], [wsb[g]])
                P.op("dve", lambda e, g=g, w=w, ug=ug, ws=ws: e.scalar_tensor_tensor(
                    out=mixed[:, g, :], in0=ug[:, 8:520], scalar=-float(w), in1=ws, op0=ALU.mult, op1=ALU.add),
                    [wsb[g], usb[i]], [mixb[g]])
                for side, cond, c0 in ((0, r0 == 0, 0), (1, r0 == 56, 504)):
                    if cond:
                        P.op("dve", lambda e, g=g, side=side, c0=c0, ws=ws: e.tensor_tensor(
                            out=t8, in0=ws[:, c0:c0 + 8], in1=invc4[:, g, side, :], op=ALU.mult),
                            [wsb[g], const_b], [t8b])
                        P.op("dve", lambda e, g=g, c0=c0, ug=ug, w=w: e.scalar_tensor_tensor(
                            out=mixed[:, g, c0:c0 + 8], in0=ug[:, 8 + c0:16 + c0], scalar=-float(w), in1=t8,
                            op0=ALU.mult, op1=ALU.add), [t8b, usb[i]], [mixb[g]])
            items = []
            for pi_ in range(4):
                r = r0 + 2 * pi_
                if r < 4:
                    kr0s = [0, 2, 4, 6]
                    edge = 1
                elif r > 58:
                    kr0s = [56, 58, 60, 62]
                    edge = 1
                else:
                    kr0s = [r - 4, r - 2, r, r + 2, r + 4]
                    edge = 0
                for h in range(8):
                    items.append((pi_, r, kr0s, edge, h))

            def qk(idx):
                pi_, r, kr0s, edge, h = items[idx]
                n = len(kr0s)
                sslot = (item + idx) % 2
                hp = slice((h % 2) * 64, (h % 2) * 64 + 64)

                def f(e):
                    ins = None
                    for j, kr0 in enumerate(kr0s):
                        c = (n - 1 - j) * 128
                        ko_ = (kr0 - klo) * 64
                        ins = e.matmul(Sap[sslot][:, c:c + 128], k_[hp, h // 2, ko_:ko_ + 128],
                                       q_[hp, h // 2, pi_ * 128:(pi_ + 1) * 128], start=True, stop=True)
                    return ins
                P.op("pe", f, [ksb[i], qsb[i]], Sb[sslot])

            qk(0)
            qk(1)
            for idx in range(32):
                pi_, r, kr0s, edge, h = items[idx]
                n = len(kr0s)
                sslot = (item + idx) % 2
                ps3 = (item + idx) % 3
                P.op("act", lambda e, sslot=sslot, n=n, ps3=ps3: e.activation(
                    out=pexp[ps3][:, 0:n * 128], in_=Sap[sslot][:, 0:n * 128], func=AF.Exp),
                    Sb[sslot], [pexpb[ps3]])
                e_start = 7 - (kr0s[-1] - r)
                toff = h * 1024 + e_start * 64
                P.op("dve", lambda e, n=n, toff=toff, edge=edge, ps3=ps3: e.tensor_tensor(
                    out=pm[ps3][:, 0:n * 128], in0=pexp[ps3][:, 0:n * 128],
                    in1=tabs[edge][:, toff:toff + n * 128], op=ALU.mult),
                    [pexpb[ps3], tabb], [pmb[ps3]])
                if idx + 2 < 32:
                    qk(idx + 2)
                ob = h // 4
                oo = (h % 4) * 65

                def pv(e, kr0s=kr0s, n=n, ps3=ps3, ob=ob, oo=oo, h=h):
                    ins = None
                    for j, kr0 in enumerate(kr0s):
                        c = (n - 1 - j) * 128
                        ins = e.matmul(Oap[ob][:, oo:oo + 65], pm[ps3][:, c:c + 128],
                                       v_[:, (kr0 - klo) // 2, h * 65:(h + 1) * 65],
                                       start=(j == 0), stop=(j == n - 1))
                    return ins
                P.op("pe", pv, [pmb[ps3], vsb[i]], [Ob[ob]])
                if h == 7:
                    for ob2 in range(2):
                        o3 = Oap[ob2][:, 0:260].rearrange("p (h f) -> p h f", h=4)
                        P.op("dve", lambda e, o3=o3, ob2=ob2: e.reciprocal(
                            out=rc[:, ob2 * 4:ob2 * 4 + 4], in_=o3[:, :, 64]), [Ob[ob2]], [rcb])
                        P.op("dve", lambda e, o3=o3, ob2=ob2: e.tensor_tensor(
                            out=ytok[:, ob2 * 256:(ob2 + 1) * 256].rearrange("p (h f) -> p h f", h=4),
                            in0=o3[:, :, 0:64],
                            in1=rc[:, ob2 * 4:ob2 * 4 + 4].unsqueeze(2).to_broadcast([128, 4, 64]), op=ALU.mult),
                            [Ob[ob2], rcb], [ytokb])

                    def tr(e):
                        ins = None
                        for c4 in range(4):
                            ins = e.transpose(psT[:, c4 * 128:(c4 + 1) * 128], ytok[:, c4 * 128:(c4 + 1) * 128], ident_bf)
                        return ins
                    P.op("pe", tr, [ytokb, const_b], [pstb])
                    P.op("act", lambda e, pi_=pi_: e.activation(
                        out=yT[:, 0:4, pi_ * 128:(pi_ + 1) * 128],
                        in_=psT[:, 0:512].rearrange("p (c t) -> p c t", c=4), func=AF.Identity), [pstb], [yTa])
                D.pop(2)
            item += 32
            D.flush()
            for g in range(4):
                pb = 4 + gp % 3
                gp += 1
                P.op("pe", lambda e, g=g, pb=pb: e.matmul(bank(pb), wpool[:, g, :], mixed[:, g, :], start=True, stop=True),
                     [woutb, mixb[g]], [psb[pb]])
                P.op("act", lambda e, g=g, pb=pb: e.activation(out=yT[:, 4 + g, :], in_=bank(pb), func=AF.Identity,
                                                               scale=vec(48 + l * 4 + g)), [psb[pb], const_b], [yTp[g]])
            for m in range(8):
                pb = 4 + gp % 3
                gp += 1

                def mo(e, m=m, pb=pb):
                    ins = None
                    for k in range(8):
                        ins = e.matmul(bank(pb), wout[:, k, m * 128:(m + 1) * 128], yT[:, k, :],
                                       start=(k == 0), stop=(k == 7))
                    return ins
                P.op("pe", mo, [woutb, yTa] + yTp, [psb[pb]])
                P.op("dve", lambda e, m=m, pb=pb: e.scalar_tensor_tensor(
                    out=xs1[:, m, :], in0=bank(pb), scalar=mv(l, 16 + m, sq), in1=xs1[:, m, :],
                    op0=ALU.mult, op1=ALU.add), [psb[pb], xsb1[m], mod_b], [xsb1[m]])

            def store(b=b):
                P.dma("sp", [(X_v[:, :, b * 512:(b + 1) * 512], os1)], osb1, [Xb[b]], osb1[0])
            fin = [store]
            if b + 1 < NBL:
                fin.append(lambda b=b: loadx(b + 1))
            pre, post = ln_steps(xs1, xsb1, 512, zbf, zsq, st, lambda m: vec(16 + l * 8 + m),
                                 lambda m: vec(32 + l * 8 + m), 1, os1, osb1, 5, 6, fin)
            for s_ in pre + post:
                D.add(s_)
        D.flush()

    def phaseC(l, final):
        A.off = PERSIST
        NT = 256
        NBC = T // NT
        w1 = A.bf16(8 * 4096).rearrange("p (k n) -> p k n", k=8)
        w2 = A.bf16(32 * 1024).rearrange("p (k n) -> p k n", k=32)
        w1b = bufs("w1", 4)
        w2b = P.buf("w2")
        for cg in range(4):
            P.dma("pool", [(w1[:, :, cg * 1024:(cg + 1) * 1024], w1_d[l, :, :, cg * 1024:(cg + 1) * 1024])],
                  [], [w1b[cg]], w1b[cg])
        P.dma("pool", [(w2[:, 4 * i:4 * i + 4, :], w2_d[l, :, 4 * i:4 * i + 4, :]) for i in range(8)], [], [w2b], w2b)
        xs = [A.f32(8 * NT).rearrange("p (m t) -> p m t", m=8) for _ in range(3)]
        xsb = [bufs("xc", 8), bufs("xc", 8), bufs("xc", 8)]
        h2 = A.bf16(8 * NT).rearrange("p (m t) -> p m t", m=8)
        h2b = bufs("h2", 8)
        hid = A.bf16(32 * NT).rearrange("p (m t) -> p m t", m=32)
        hidb = bufs("hid", 32)
        rl = [A.f32(NT), A.f32(NT)]
        rlb = [P.buf("rl"), P.buf("rl")]
        os_ = [A.f32(8 * NT).rearrange("p (m t) -> p m t", m=8) for _ in range(2)]
        osb = [bufs("oc", 8), bufs("oc", 8)]
        zbf = (A.bf16(8 * NT).rearrange("p (m t) -> p m t", m=8), bufs("zbf", 8))
        zsq = (A.bf16(8 * NT).rearrange("p (m t) -> p m t", m=8), bufs("zsq", 8))
        st = [A.f32(NT) for _ in range(4)]
        dst_v = outT_v if final else X_v
        NBL = NBC if lim is None else 2 * (lim - 1)
        D = Deferred()

        def load(b):
            P.dma("sp", [(xs[b % 3], X_v[:, :, b * NT:(b + 1) * NT])], [Xb[(b * NT) // 512]], xsb[b % 3],
                  xsb[b % 3][0])

        def hcomp(b):
            sq = (b * NT) // SEQ
            x_ = xs[b % 3]
            for m in range(8):
                if m % 2 == 0:
                    P.op("act", lambda e, m=m: e.activation(out=h2[:, m, :], in_=x_[:, m, :], func=AF.Identity,
                                                            bias=mv(l, 24 + m, sq), scale=mv(l, 32 + m, sq)),
                         [xsb[b % 3][m], mod_b], [h2b[m]])
                else:
                    P.op("dve", lambda e, m=m: e.tensor_scalar(out=h2[:, m, :], in0=x_[:, m, :],
                                                               scalar1=mv(l, 32 + m, sq), scalar2=mv(l, 24 + m, sq),
                                                               op0=ALU.mult, op1=ALU.add),
                         [xsb[b % 3][m], mod_b], [h2b[m]])
        load(0)
        hcomp(0)
        pi = 0
        for b in range(NBL):
            i = b % 2
            sq = (b * NT) // SEQ
            if b + 1 < NBL:
                load(b + 1)
            x_ = xs[b % 3]
            xb_ = xsb[b % 3]
            for jc in range(32):
                pb = pi % 5
                pi += 1

                def m1(e, jc=jc, pb=pb):
                    ins = None
                    for k in range(8):
                        ins = e.matmul(bank(pb)[:, 0:NT], w1[:, k, jc * 128:(jc + 1) * 128], h2[:, k, :],
                                       start=(k == 0), stop=(k == 7))
                    return ins
                P.op("pe", m1, [w1b[jc // 8]] + h2b, [psb[pb]])
                ri = jc % 2
                P.op("act", lambda e, pb=pb, ri=ri: e.activation(out=rl[ri], in_=bank(pb)[:, 0:NT], func=AF.Relu),
                     [psb[pb]], [rlb[ri]])
                P.op("pool", lambda e, jc=jc, ri=ri: e.tensor_tensor(out=hid[:, jc, :], in0=rl[ri], in1=rl[ri],
                                                                     op=ALU.mult), [rlb[ri]], [hidb[jc]])
                D.pop(2)
            D.flush()
            if b + 1 < NBL:
                hcomp(b + 1)
            for m in range(8):
                pb = pi % 5
                pi += 1

                def m2(e, m=m, pb=pb):
                    ins = None
                    for k in range(32):
                        ins = e.matmul(bank(pb)[:, 0:NT], w2[:, k, m * 128:(m + 1) * 128], hid[:, k, :],
                                       start=(k == 0), stop=(k == 31))
                    return ins
                P.op("pe", m2, [w2b] + hidb, [psb[pb]])
                P.op("dve", lambda e, m=m, pb=pb: e.scalar_tensor_tensor(
                    out=x_[:, m, :], in0=bank(pb)[:, 0:NT], scalar=mv(l, 40 + m, sq), in1=x_[:, m, :],
                    op0=ALU.mult, op1=ALU.add), [psb[pb], xb_[m], mod_b], [xb_[m]])
            wr = [] if final else [Xb[(b * NT) // 512]]

            def store(b=b, i=i, wr=wr):
                P.dma("sp", [(dst_v[:, :, b * NT:(b + 1) * NT], os_[i])], osb[i], wr, osb[i][0])
            pre, post = ln_steps(x_, xb_, NT, zbf, zsq, st, lambda m: vec(56 + l * 8 + m),
                                 lambda m: vec(72 + l * 8 + m), 1, os_[i], osb[i], 5, 6, [store])
            for s_ in pre + post:
                D.add(s_)
        D.flush()

    phases = [("0", phase0)]
    for l in range(L):
        phases += [("A%d" % l, lambda l=l: phaseA(l)), ("B%d" % l, lambda l=l: phaseB(l)),
                   ("C%d" % l, lambda l=l: phaseC(l, l == L - 1))]
    for name, fn in phases:
        P.barrier()
        fn()
        if stop_after == name:
            break
    P.barrier()
    return nc


def _chunk_vec(v):
    return np.ascontiguousarray(v.reshape(-1, 128).T)


def _bias_tables(rpb_l):
    out = np.full((2, 128, 8, 16, 64), NEG, np.float32)
    c = np.arange(64)
    cs = np.clip(c - 8, 0, 48)
    for kl in range(2):
        for e in range(16):
            rel = 14 - e + kl
            if rel < 0 or rel > 14:
                continue
            for kc in range(64):
                valid = (kc >= cs) & (kc < cs + 16)
                idx = kc - c + 15
                cc = c[valid]
                vals = rpb_l[:, rel, idx[valid]]
                out[1, kl * 64 + kc, :, e, cc] = vals.T
                if 3 <= rel <= 10:
                    out[0, kl * 64 + kc, :, e, cc] = vals.T
    return out.reshape(2, 128, 8192)


def _prep_shared(inp):
    sh = {}
    sh["w_ada"] = np.ascontiguousarray(inp["w_ada"].reshape(L, 8, 128, 6 * D).transpose(0, 2, 1, 3))
    sh["b_ada"] = np.ascontiguousarray(inp["b_ada"].reshape(L, 48, 128).transpose(2, 0, 1))
    sh["w_in"] = np.ascontiguousarray(inp["w_in"].reshape(L, 8, 128, 2048).transpose(0, 2, 1, 3))
    sh["w_out"] = np.ascontiguousarray(inp["w_out"].reshape(L, 8, 128, 1024).transpose(0, 2, 1, 3))
    sh["w_mlp1"] = np.ascontiguousarray(inp["w_mlp1"].reshape(L, 8, 128, 4096).transpose(0, 2, 1, 3))
    sh["w_mlp2"] = np.ascontiguousarray(inp["w_mlp2"].reshape(L, 32, 128, 1024).transpose(0, 2, 1, 3))
    sh["w_pool"] = np.ascontiguousarray(inp["w_pool"].transpose(0, 2, 1, 3))
    vecs = np.zeros((128, 88), np.float32)
    vecs[:, 0:8] = _chunk_vec(inp["ln_in_g"])
    vecs[:, 8:16] = _chunk_vec(inp["ln_in_b"])
    for l in range(L):
        vecs[:, 16 + l * 8:24 + l * 8] = _chunk_vec(inp["ln1_g"][l])
        vecs[:, 32 + l * 8:40 + l * 8] = _chunk_vec(inp["ln1_b"][l])
        vecs[:, 48 + l * 4:52 + l * 4] = inp["pool_scale"][l].reshape(4, 128).T
        vecs[:, 56 + l * 8:64 + l * 8] = _chunk_vec(inp["ln2_g"][l])
        vecs[:, 72 + l * 8:80 + l * 8] = _chunk_vec(inp["ln2_b"][l])
    sh["vecs"] = vecs
    sh["biasT"] = np.stack([_bias_tables(np.asarray(inp["rpb"][l])) for l in range(L)])
    sh["ident"] = np.eye(128, dtype=np.float32)
    invc = np.zeros((4, 2, 8), np.float32)
    for g, w in enumerate((2, 4, 8, 16)):
        for side, toks in ((0, np.arange(0, 8)), (1, np.arange(SEQ - 8, SEQ))):
            lo = np.clip(toks - w // 2, 0, SEQ)
            hi = np.clip(toks - w // 2 + w, 0, SEQ)
            invc[g, side] = w / (hi - lo)
    sh["invc"] = np.ascontiguousarray(np.broadcast_to(invc.reshape(1, 64), (128, 64)))
    return sh


VEC_COLS = 88


def _core_inputs(inp, sh, core):
    b0 = core * NSEQ
    x2 = np.asarray(inp["x"][b0:b0 + NSEQ]).reshape(T, D)
    m = dict(sh)
    m["xT"] = np.ascontiguousarray(x2.T)
    c2 = np.asarray(inp["c"][b0:b0 + NSEQ])
    m["cT"] = np.ascontiguousarray(c2.reshape(NSEQ, 8, 128).transpose(2, 1, 0))
    return m


_NC_CACHE = {}


def kernel(**inputs):
    inp = {k: np.asarray(v) for k, v in inputs.items()}
    sh = _prep_shared(inp)
    if "nc" not in _NC_CACHE:
        _NC_CACHE["nc"] = build_nc()
    nc = _NC_CACHE["nc"]
    in_maps = [_core_inputs(inp, sh, c) for c in range(NCORES)]
    res = run_bass_kernel_spmd(nc, in_maps, core_ids=list(range(NCORES)))
    out = np.empty((NCORES * NSEQ, SEQ, D), np.float32)
    for c in range(NCORES):
        oT = np.asarray(res.results[c]["outT"])
        out[c * NSEQ:(c + 1) * NSEQ] = oT.T.reshape(NSEQ, SEQ, D)
    return out
```
